# Optimizing a Trainium2 kernel written in Bass

```python
import math
import jax, jax.numpy as jnp
from jax import lax
import numpy as np

D_MODEL = 1024
BATCH = 8
SEQ = 2048
DEPTH = 4

D_MIX = D_MODEL
N_MIXERS = 4
W_GRP = D_MIX // N_MIXERS
SGU_HEADS = 4
SGU_HEAD_DIM = W_GRP // SGU_HEADS
CHUNK = 128
POOL_WINDOWS = (2, 4, 8, 16)
POOL_GROUPS = len(POOL_WINDOWS)
POOL_GROUP_DIM = W_GRP // POOL_GROUPS
CONV_WIDTH = 3
S5_GROUP_CH = 16
S5_GROUPS = W_GRP // S5_GROUP_CH
S5_STATE = 64
P_IN = 2 * W_GRP + W_GRP + 3 * W_GRP + W_GRP
D_FF = ((8 * D_MODEL // 3 + 255) // 256) * 256
N_ADA = 9
EPS = 1e-6

kernel_name = "hybrid_parallel_mixer_trunk"


def rmsnorm(x, g):
    xf = x.astype(jnp.float32)
    y = xf * lax.rsqrt(jnp.mean(xf * xf, axis=-1, keepdims=True) + EPS)
    return (y * g.astype(jnp.float32)).astype(x.dtype)


def group_rmsnorm(y, g):
    b, s, _ = y.shape
    yf = y.astype(jnp.float32).reshape(b, s, N_MIXERS, W_GRP)
    yf = yf * lax.rsqrt(jnp.mean(yf * yf, axis=-1, keepdims=True) + EPS)
    return (yf.reshape(b, s, D_MIX) * g.astype(jnp.float32)).astype(y.dtype)


def modulate(h, shift, scale):
    return h * (1.0 + scale) + shift


def swiglu(h, w_in, w_out):
    a, b = jnp.split(h @ w_in, 2, axis=-1)
    return (jax.nn.silu(a) * b) @ w_out


def sgu_mixer(z, w_s, b_s):
    bsz, s, _ = z.shape
    z = jax.nn.gelu(z)
    u, v = jnp.split(z, 2, axis=-1)
    v = v.reshape(bsz, s // CHUNK, CHUNK, SGU_HEADS, SGU_HEAD_DIM)
    vf = v.astype(jnp.float32)
    mu = jnp.mean(vf, axis=-1, keepdims=True)
    var = jnp.mean(jnp.square(vf - mu), axis=-1, keepdims=True)
    v = ((vf - mu) * lax.rsqrt(var + EPS)).astype(z.dtype)
    mask = jnp.tril(jnp.ones((CHUNK, CHUNK), dtype=w_s.dtype))
    mixed = jnp.einsum('hts,bnshd->bnthd', w_s * mask, v)
    mixed = mixed + b_s.T[None, None, :, :, None]
    return u * mixed.reshape(bsz, s, W_GRP)


def pool_mixer(z, w_p, scale):
    bsz, s, _ = z.shape
    zf = z.astype(jnp.float32).reshape(bsz, s, POOL_GROUPS, POOL_GROUP_DIM)
    cs = jnp.concatenate([jnp.zeros_like(zf[:, :1]), jnp.cumsum(zf, axis=1)], axis=1)
    t = jnp.arange(s)
    win = jnp.array(POOL_WINDOWS, dtype=jnp.int32)
    lo = jnp.maximum(t[:, None] + 1 - win[None, :], 0)
    cnt = (t[:, None] + 1 - lo).astype(jnp.float32)
    lower = cs[:, lo, jnp.arange(POOL_GROUPS)[None, :]]
    mean = (cs[:, 1:] - lower) / cnt[None, :, :, None]
    p = (mean - zf).astype(z.dtype)
    out = jnp.einsum('bsgc,gcd->bsgd', p, w_p).reshape(bsz, s, W_GRP)
    return out * scale


def conv_mixer(z, conv_w):
    bg, cg, xh = jnp.split(z, 3, axis=-1)
    y = cg * xh
    y = lax.conv_general_dilated(
        y, conv_w[:, None, :], window_strides=(1,), padding=[(CONV_WIDTH - 1, 0)],
        dimension_numbers=('NWC', 'WIO', 'NWC'), feature_group_count=W_GRP)
    return bg * y


def s5_mixer(u, lam_re, lam_im, b_re, b_im, c_re, c_im, d, log_dt, glu_w, glu_b):
    bsz, s, _ = u.shape
    f32 = jnp.float32
    dt = jnp.exp(log_dt.astype(f32))[:, None]
    lre, lim = lam_re.astype(f32), lam_im.astype(f32)
    mag = jnp.exp(lre * dt)
    ang = lim * dt
    a_re, a_im = mag * jnp.cos(ang), mag * jnp.sin(ang)
    nr, ni = a_re - 1.0, a_im
    den = lre * lre + lim * lim
    k_re = (nr * lre + ni * lim) / den
    k_im = (ni * lre - nr * lim) / den
    br, bi = b_re.astype(f32), b_im.astype(f32)
    bb_re = k_re[..., None] * br - k_im[..., None] * bi
    bb_im = k_re[..., None] * bi + k_im[..., None] * br
    uf = u.astype(f32)
    ug = uf.reshape(bsz, s, S5_GROUPS, S5_GROUP_CH)
    bu_re = jnp.einsum('bsgc,gpc->bsgp', ug, bb_re)
    bu_im = jnp.einsum('bsgc,gpc->bsgp', ug, bb_im)
    ar = jnp.broadcast_to(a_re, bu_re.shape)
    ai = jnp.broadcast_to(a_im, bu_re.shape)

    def combine(e1, e2):
        a1r, a1i, b1r, b1i = e1
        a2r, a2i, b2r, b2i = e2
        return (a2r * a1r - a2i * a1i,
                a2r * a1i + a2i * a1r,
                a2r * b1r - a2i * b1i + b2r,
                a2r * b1i + a2i * b1r + b2i)

    _, _, xr, xi = lax.associative_scan(combine, (ar, ai, bu_re, bu_im), axis=1)
    y = (jnp.einsum('gcp,bsgp->bsgc', c_re.astype(f32), xr)
         - jnp.einsum('gcp,bsgp->bsgc', c_im.astype(f32), xi))
    y = y.reshape(bsz, s, W_GRP) + d.astype(f32) * uf
    y = jax.nn.gelu(y).astype(u.dtype)
    return y * jax.nn.sigmoid(y @ glu_w + glu_b)


def setup_inputs(seed: int = 0) -> dict:
    key = jax.random.key(seed)
    ks = jax.random.split(key, 32)
    f32 = jnp.float32
    L, D = DEPTH, D_MODEL

    def nrm(k, shape, scale):
        return jax.random.normal(k, shape, f32) * scale

    def gain(k, shape):
        return 1.0 + 0.05 * jax.random.normal(k, shape, f32)

    lam_im0 = jnp.broadcast_to(math.pi * jnp.arange(S5_STATE, dtype=f32), (L, S5_GROUPS, S5_STATE))
    return {
        "x": nrm(ks[0], (BATCH, SEQ, D), 1.0),
        "c": nrm(ks[1], (BATCH, D), 1.0),
        "ada_w": nrm(ks[2], (L, D, N_ADA * D), 0.5 * D ** -0.5),
        "ada_b": nrm(ks[3], (L, N_ADA * D), 0.01),
        "norm1_g": gain(ks[4], (L, D)),
        "ffn1_w_in": nrm(ks[5], (L, D, 2 * D_FF), D ** -0.5),
        "ffn1_w_out": nrm(ks[6], (L, D_FF, D), D_FF ** -0.5),
        "norm2_g": gain(ks[7], (L, D)),
        "w_mix_in": nrm(ks[8], (L, D, P_IN), D ** -0.5),
        "sgu_w": nrm(ks[9], (L, SGU_HEADS, CHUNK, CHUNK), CHUNK ** -0.5),
        "sgu_b": 1.0 + nrm(ks[10], (L, SGU_HEADS, CHUNK), 0.1),
        "pool_w": nrm(ks[11], (L, POOL_GROUPS, POOL_GROUP_DIM, POOL_GROUP_DIM), POOL_GROUP_DIM ** -0.5),
        "pool_scale": 1.0 + nrm(ks[12], (L, W_GRP), 0.1),
        "conv_w": nrm(ks[13], (L, CONV_WIDTH, W_GRP), CONV_WIDTH ** -0.5),
        "s5_lambda_re": -0.5 + nrm(ks[14], (L, S5_GROUPS, S5_STATE), 0.01),
        "s5_lambda_im": lam_im0 + nrm(ks[15], (L, S5_GROUPS, S5_STATE), 0.01),
        "s5_b_re": nrm(ks[16], (L, S5_GROUPS, S5_STATE, S5_GROUP_CH), (2 * S5_GROUP_CH) ** -0.5),
        "s5_b_im": nrm(ks[17], (L, S5_GROUPS, S5_STATE, S5_GROUP_CH), (2 * S5_GROUP_CH) ** -0.5),
        "s5_c_re": nrm(ks[18], (L, S5_GROUPS, S5_GROUP_CH, S5_STATE), (2 * S5_STATE) ** -0.5),
        "s5_c_im": nrm(ks[19], (L, S5_GROUPS, S5_GROUP_CH, S5_STATE), (2 * S5_STATE) ** -0.5),
        "s5_d": nrm(ks[20], (L, W_GRP), 1.0),
        "s5_log_dt": jax.random.uniform(ks[21], (L, S5_GROUPS), f32, math.log(1e-3), math.log(1e-1)),
        "s5_glu_w": nrm(ks[22], (L, W_GRP, W_GRP), W_GRP ** -0.5),
        "s5_glu_b": nrm(ks[23], (L, W_GRP), 0.01),
        "mix_norm_g": gain(ks[24], (L, D_MIX)),
        "w_mix_out": nrm(ks[25], (L, D_MIX, D), D_MIX ** -0.5),
        "norm3_g": gain(ks[26], (L, D)),
        "ffn2_w_in": nrm(ks[27], (L, D, 2 * D_FF), D ** -0.5),
        "ffn2_w_out": nrm(ks[28], (L, D_FF, D), D_FF ** -0.5),
        "final_norm_g": gain(ks[29], (D,)),
    }


def reference(x, c, ada_w, ada_b, norm1_g, ffn1_w_in, ffn1_w_out, norm2_g, w_mix_in,
              sgu_w, sgu_b, pool_w, pool_scale, conv_w, s5_lambda_re, s5_lambda_im,
              s5_b_re, s5_b_im, s5_c_re, s5_c_im, s5_d, s5_log_dt, s5_glu_w, s5_glu_b,
              mix_norm_g, w_mix_out, norm3_g, ffn2_w_in, ffn2_w_out, final_norm_g):
    c_act = jax.nn.silu(c)
    for l in range(DEPTH):
        cond = (c_act @ ada_w[l] + ada_b[l])[:, None, :]
        sh1, sc1, g1, sh2, sc2, g2, sh3, sc3, g3 = jnp.split(cond, N_ADA, axis=-1)

        h = modulate(rmsnorm(x, norm1_g[l]), sh1, sc1)
        x = x + 0.5 * g1 * swiglu(h, ffn1_w_in[l], ffn1_w_out[l])

        h = modulate(rmsnorm(x, norm2_g[l]), sh2, sc2)
        z = h @ w_mix_in[l]
        za, zb, zc, zd = jnp.split(z, [2 * W_GRP, 3 * W_GRP, 6 * W_GRP], axis=-1)
        ya = sgu_mixer(za, sgu_w[l], sgu_b[l])
        yb = pool_mixer(zb, pool_w[l], pool_scale[l])
        yc = conv_mixer(zc, conv_w[l])
        yd = s5_mixer(zd, s5_lambda_re[l], s5_lambda_im[l], s5_b_re[l], s5_b_im[l],
                      s5_c_re[l], s5_c_im[l], s5_d[l], s5_log_dt[l], s5_glu_w[l], s5_glu_b[l])
        y = group_rmsnorm(jnp.concatenate([ya, yb, yc, yd], axis=-1), mix_norm_g[l])
        x = x + g2 * (y @ w_mix_out[l])

        h = modulate(rmsnorm(x, norm3_g[l]), sh3, sc3)
        x = x + 0.5 * g3 * swiglu(h, ffn2_w_in[l], ffn2_w_out[l])
    return rmsnorm(x, final_norm_g)
```

```python
import math
from contextlib import ExitStack

import numpy as np
import concourse.bass as bass
import concourse.mybir as mybir
from concourse.bass_utils import run_bass_kernel_spmd

F32 = mybir.dt.float32
BF16 = mybir.dt.bfloat16
AF = mybir.ActivationFunctionType
ALU = mybir.AluOpType
AX = mybir.AxisListType

L = 4
D = 1024
S = 2048
DFF = 2816
NM = 22
PIN = 1792
EPS = 1e-6
TWO_PI = 2.0 * math.pi
MAGIC = 12582912.0
NPV = 68
PV_N1, PV_N2, PV_N3, PV_MNG, PV_PSC, PV_CW, PV_D, PV_GB, PV_LRE, PV_LIM, PV_LDT = 0, 8, 16, 24, 32, 34, 40, 42, 44, 52, 60
C_ID, C_TRI, C_IOTA, C_ICNT, C_IW, NCST = 0, 128, 256, 512, 544, 546

DEBUG = {}


class Prog:
    ENGS = ("pe", "act", "dve", "pool", "sp")

    def __init__(self):
        self.ops = {e: [] for e in self.ENGS}
        self.cnt = {e: 0 for e in self.ENGS}
        self.epoch = 0
        self.semkeys = []
        self.dcnt = {}
        self.last_w = {}
        self.readers = {}
        self.waited = {e: {} for e in self.ENGS}
        self.alias = {}

    def _exp(self, toks):
        out = []
        for t in toks:
            a = self.alias.get(t)
            if a is None:
                out.append(t)
            else:
                out.extend(a)
        return out

    def new_epoch(self):
        self.epoch += 1
        for e in self.ENGS:
            self.cnt[e] = 0

    def _sk(self, k):
        if k not in self.dcnt:
            self.dcnt[k] = 0
            self.semkeys.append(k)
        return k

    def _waits(self, eng, reads, writes):
        need = {}

        def add(ev, is_read):
            peng, sk, val = ev
            if peng == eng and eng == "pe" and not is_read:
                return
            if need.get(sk, 0) < val:
                need[sk] = val

        for t in reads:
            w = self.last_w.get(t)
            if w is not None:
                add(w, True)
        for t in writes:
            w = self.last_w.get(t)
            if w is not None:
                add(w, False)
            for ev in self.readers.get(t, {}).values():
                add(ev, False)
        out = []
        wd = self.waited[eng]
        for sk, val in need.items():
            if wd.get(sk, 0) < val:
                wd[sk] = val
                out.append((sk, val))
        return out

    def _record(self, ev, reads, writes):
        for t in reads:
            d = self.readers.setdefault(t, {})
            old = d.get(ev[1])
            if old is None or old[2] < ev[2]:
                d[ev[1]] = ev
        for t in writes:
            self.last_w[t] = ev
            self.readers[t] = {}

    def op(self, eng, reads, writes, fn):
        reads = self._exp(reads)
        writes = self._exp(writes)
        waits = self._waits(eng, reads, writes)
        sk = self._sk(("E", eng, self.epoch))
        self.cnt[eng] += 1
        self.dcnt[sk] = self.cnt[eng]
        ev = (eng, sk, self.cnt[eng])
        self.ops[eng].append(("op", waits, fn, sk))
        self._record(ev, reads, writes)
        return ev

    def dma(self, eng, semkey, reads, writes, pairs):
        reads = self._exp(reads)
        writes = self._exp(writes)
        waits = self._waits(eng, reads, writes)
        sk = self._sk(("D", semkey))
        self.dcnt[sk] += 16 * len(pairs)
        ev = (None, sk, self.dcnt[sk])
        self.ops[eng].append(("dma", waits, pairs, sk))
        self._record(ev, reads, writes)
        return ev

    def wait_all(self, eng, evs):
        waits = []
        for (_, sk, val) in evs:
            waits.append((sk, val))
        self.ops[eng].append(("wait", waits, None, None))

    def emit(self, nc, block, sems):
        engmap = {"pe": block.tensor, "act": block.scalar, "dve": block.vector,
                  "pool": block.gpsimd, "sp": block.sync}
        for eng in self.ENGS:
            ops = self.ops[eng]
            if not ops:
                continue

            def body(e, ops=ops):
                for kind, waits, fn, sk in ops:
                    for (wsk, val) in waits:
                        e.wait_ge(sems[wsk], val)
                    if kind == "op":
                        ins = fn(e)
                        ins.then_inc(sems[sk], 1)
                    elif kind == "dma":
                        for (o, i) in fn:
                            e.dma_start(out=o, in_=i).then_inc(sems[sk], 16)

            engmap[eng](body)


class Ring:
    def __init__(self, P, name, nslots, chunks):
        self.P = P
        self.name = name
        self.n = nslots
        self.chunks = chunks
        self.loaded = 0
        self.cur = 0

    def _emit_load(self, i):
        slot = i % self.n
        ch = self.chunks[i]
        self.P.dma("pool", (self.name, slot), [], [(self.name, slot)], ch["pairs"](slot))

    def prefetch(self):
        hi = min(self.cur + self.n - 1, len(self.chunks) - 1)
        j = self.loaded
        while j <= hi:
            if self.chunks[j].get("barrier"):
                break
            self._emit_load(j)
            j += 1
        self.loaded = max(self.loaded, j)

    def next(self, tag, la=None):
        i = self.cur
        assert self.chunks[i]["tag"] == tag, (self.chunks[i]["tag"], tag)
        if la is None:
            la = self.n - 1
        hi = min(i + la, len(self.chunks) - 1)
        j = self.loaded
        while j <= hi:
            if j > i and self.chunks[j].get("barrier"):
                break
            self._emit_load(j)
            j += 1
        self.loaded = max(self.loaded, j)
        self.cur += 1
        return i % self.n


def build_program(n_layers=L, debug_taps=()):
    nc = bass.Bass("TRN2", target_bir_lowering=False)
    P = Prog()
    for b in range(8):
        P.alias[("psh", b, 0)] = [("ps", b)]
        P.alias[("psh", b, 1)] = [("ps", b)]
    for a in range(4):
        P.alias[("tA", a)] = [("tA8", 2 * a), ("tA8", 2 * a + 1)]
        P.alias[("tB", a)] = [("tB8", 2 * a), ("tB8", 2 * a + 1)]

    def din(name, shape):
        return nc.dram_tensor(name, list(shape), F32, kind="ExternalInput").ap()

    x_d = din("x", [S, D])
    cT_d = din("cT", [128, 8])
    cst_d = din("cst", [128, NCST])
    fng_d = din("fng", [128, 8])
    ada_w_d = din("ada_w", [L, D, 9 * D])
    ada_bT_d = din("ada_bT", [L, 128, 72])
    pvec_d = din("pvec", [L, 128, NPV])
    w1i_d = din("ffn1_w_in", [L, D, 2 * DFF])
    w1o_d = din("ffn1_w_out", [L, DFF, D])
    w2i_d = din("ffn2_w_in", [L, D, 2 * DFF])
    w2o_d = din("ffn2_w_out", [L, DFF, D])
    wmi_d = din("w_mix_in", [L, D, PIN])
    wmo_d = din("w_mix_out", [L, D, D])
    sguw_d = din("sgu_w", [L, 4, 128, 128])
    sgub_d = din("sgu_bT", [L, 128, 2, 128])
    pool_d = din("pool_bd", [L, 128, 2, 128])
    bp_d = din("s5_Bp", [L, 2, 128, 8, 128])
    cp_d = din("s5_Cp", [L, 2, 128, 8, 128])
    gluw_d = din("glu_w", [L, 256, 256])
    out_d = nc.dram_tensor("out", [S, D], F32, kind="ExternalOutput").ap()
    taps_d = {}
    for (nm, shp) in debug_taps:
        taps_d[nm] = nc.dram_tensor("tap_" + nm, list(shp), F32, kind="ExternalOutput").ap()

    es = ExitStack()

    def sb(name, shape, dt):
        return es.enter_context(nc.sbuf_tensor("sb_" + name, list(shape), dt))

    xT = sb("xT", [128, 8, S], F32)
    hT = sb("hT", [128, 8, S], BF16)
    G = sb("G", [128, 8, S], BF16)
    RA = sb("RA", [128, 3, 8, 512], BF16)
    RB = sb("RB", [128, 4, 8, 256], BF16)
    tA = sb("tA", [128, 2048], F32)
    tB = sb("tB", [128, 4, 512], BF16)
    rs = sb("rs", [128, 2, 512], F32)
    tC = sb("tC", [128, 4, 528], F32)
    cst = sb("cst", [128, NCST], F32)
    ident_b = sb("ident_b", [128, 128], BF16)
    ones_b = sb("ones_b", [128, 128], BF16)
    epsc = sb("epsc", [128, 1], F32)
    cT_f = sb("cT_f", [128, 8], F32)
    cact_b = sb("cact_b", [128, 8], BF16)
    condT = sb("condT", [128, 2, 72], F32)
    adab = sb("adab", [128, 2, 72], F32)
    modA = sb("modA", [128, 2, 3, 8], F32)
    gate = sb("gate", [128, 2, 3, 8], F32)
    pvec = sb("pvec", [128, 2, NPV], F32)
    fng = sb("fng", [128, 8], F32)
    WsT = sb("WsT", [128, 4, 128], BF16)
    bsT = sb("bsT", [128, 2, 128], F32)
    poolw = sb("poolw", [128, 2, 128], BF16)
    gluw = sb("gluw", [128, 2, 256], BF16)
    diagD = sb("diagD", [128, 2, 128], BF16)
    s5s = sb("s5s", [128, 16, 8], F32)
    s5i = sb("s5i", [128, 2, 8], F32)
    smal = sb("smal", [128, 40], F32)
    nCTim = sb("nCTim", [128, 8, 128], BF16)
    crowd = sb("crowd", [1, 512], F32)
    g5sq = sb("g5sq", [128, 2, 256], BF16)
    g5rs = sb("g5rs", [128, 256], F32)

    psb = [es.enter_context(nc.psum_tensor("ps%d" % i, [128, 512], F32)) for i in range(8)]

    ident_f = cst[:, C_ID:C_ID + 128]
    trimask = cst[:, C_TRI:C_TRI + 128]
    iota = cst[:, C_IOTA:C_IOTA + 256]
    one_f = cst[0:1, 0:1]

    Gf = G[:].rearrange("p a b -> p (a b)").bitcast(F32)
    RAf = RA[:].rearrange("p a b c -> p (a b c)").bitcast(F32)
    tab = RAf[:, 0:4096].rearrange("p (i c t) -> p i c t", i=8, c=2)
    RA2 = RA[:, 2].rearrange("p a b -> p (a b)")
    BbT = [RA2[:, 0:1024].rearrange("p (i c) -> p i c", i=8), RA2[:, 1024:2048].rearrange("p (i c) -> p i c", i=8)]
    CTb = [RA2[:, 2048:3072].rearrange("p (i c) -> p i c", i=8), RA2[:, 3072:4096].rearrange("p (i c) -> p i c", i=8)]

    def sl(s, w=512):
        return slice(s * w, (s + 1) * w)

    ring_state = {"AB": 0, "OUT": 0}

    def ab_pair():
        i = ring_state["AB"]
        ring_state["AB"] = (i + 1) % 2
        return 2 * i, 2 * i + 1

    def out_bank():
        i = ring_state["OUT"]
        ring_state["OUT"] = (i + 1) % 2
        return 4 + i

    FFN_PARTS = [(0, 8), (8, 8), (16, 6)]

    def ra_pairs_win(wd, l, m):
        def f(slot):
            src = wd[l].rearrange("(k p) n -> p k n", p=128)
            return [(RA[:, slot, :, 0:256], src[:, :, m * 128:m * 128 + 256]),
                    (RA[:, slot, :, 256:512], src[:, :, DFF + m * 128:DFF + m * 128 + 256])]
        return f

    def ra_pairs_ada(l, cc):
        def f(slot):
            src = ada_w_d[l].rearrange("(k p) n -> p k n", p=128)
            return [(RA[:, slot, :, :], src[:, :, cc * 512:(cc + 1) * 512])]
        return f

    def rb_pairs_wout(wd, l, m0, mc, jp):
        def f(slot):
            src = wd[l].rearrange("(m p) n -> p m n", p=128)
            return [(RB[:, slot, 0:mc, :], src[:, m0:m0 + mc, jp * 256:(jp + 1) * 256])]
        return f

    def rb_pairs_cols(wd, l, c0):
        def f(slot):
            src = wd[l].rearrange("(k p) n -> p k n", p=128)
            return [(RB[:, slot, :, :], src[:, :, c0:c0 + 256])]
        return f

    MIX_CHUNKS = [("D", 1536), ("Au", 0), ("Av", 256), ("B", 512), ("Cc", 1024), ("Cx", 1280), ("Cb", 768)]

    def ada_after(l, f, ci):
        out = []
        if l == 0 and f == 0:
            out.append((0, 6 + ci))
            if ci == 10:
                out.append((0, 17))
        if l + 1 < n_layers and ci < 9:
            out.append((l + 1, f * 9 + ci))
        return out

    ra_chunks = []
    rb_chunks = []
    for cc in range(6):
        ra_chunks.append(dict(tag=("ada", 0, cc), pairs=ra_pairs_ada(0, cc)))
    for l in range(n_layers):
        for f in range(2):
            wi = w1i_d if f == 0 else w2i_d
            wo = w1o_d if f == 0 else w2o_d
            ci = 0
            first = True
            for (m0, mc) in FFN_PARTS:
                for c in range(mc // 2):
                    ra_chunks.append(dict(tag=("win", l, f, m0 + 2 * c), pairs=ra_pairs_win(wi, l, m0 + 2 * c),
                                          barrier=(first and f == 1)))
                    first = False
                    for (al, acc) in ada_after(l, f, ci):
                        ra_chunks.append(dict(tag=("ada", al, acc), pairs=ra_pairs_ada(al, acc)))
                    ci += 1
                for jp in range(4):
                    rb_chunks.append(dict(tag=("wout", l, f, m0, jp), pairs=rb_pairs_wout(wo, l, m0, mc, jp)))
            if f == 0:
                for (nm, c0) in MIX_CHUNKS:
                    rb_chunks.append(dict(tag=("mi", l, nm), pairs=rb_pairs_cols(wmi_d, l, c0)))
                for jp in range(4):
                    rb_chunks.append(dict(tag=("mo", l, jp), pairs=rb_pairs_cols(wmo_d, l, jp * 256)))
    RAr = Ring(P, "RA", 3, ra_chunks)
    RBr = Ring(P, "RB", 4, rb_chunks)

    def tx(j, s): return ("x", j, s)
    def th(k, s): return ("h", k, s)
    def tg(m, s): return ("G", m, s)
    def tps(b): return ("ps", b)

    P.dma("sp", "c0a", [], ["cst"], [(cst[:], cst_d)])
    P.dma("sp", "c0b", [], ["cT_f"], [(cT_f[:], cT_d)])
    P.dma("sp", "c0c", [], ["fng"], [(fng[:], fng_d)])
    P.op("dve", [], ["ones_b"], lambda e: e.memset(ones_b[:], 1.0))
    P.op("dve", [], ["epsc"], lambda e: e.memset(epsc[:], EPS))
    P.op("dve", ["cst"], ["ident_b"], lambda e: e.tensor_copy(out=ident_b[:], in_=ident_f))
    P.op("act", ["cT_f"], ["cact_b"], lambda e: e.activation(out=cact_b[:], in_=cT_f[:], func=AF.Silu))

    def load_layer_params(l):
        par = l % 2
        P.dma("sp", ("pv", par), [], [("pvec", par)], [(pvec[:, par, :], pvec_d[l])])
        P.dma("sp", ("ab", par), [], [("adab", par)], [(adab[:, par, :], ada_bT_d[l])])

    crow = crowd[0:1, 0:512]
    cond_pending = []

    def cond_flush():
        while cond_pending:
            cond_pending.pop(0)()

    def cond_chunk(l, cc):
        cond_flush()
        slot = RAr.next(("ada", l, cc))

        def f1(e):
            ins = None
            for k in range(8):
                ins = e.matmul(psb[6][0:1, :], cact_b[:, k:k + 1], RA[:, slot, k, :], start=(k == 0), stop=(k == 7))
            return ins
        P.op("pe", [("RA", slot), "cact_b"], [tps(6)], f1)
        P.op("act", [tps(6)], ["crow"], lambda e: e.activation(out=crow, in_=psb[6][0:1, :], func=AF.Copy))

        def f2(e):
            ins = None
            for q in range(4):
                col = cc * 4 + q
                ins = e.matmul(psb[7][:, col:col + 1], crow[0:1, q * 128:(q + 1) * 128], one_f, start=True, stop=True)
            return ins
        cond_pending.append(lambda: P.op("pe", ["crow", "cst"], [("psC",)], f2))

    def cond_finish_n(l, n):
        par = l % 2
        c0, c1 = 24 * n, 24 * n + 24
        P.op("dve", [("psC",), ("adab", par)], [("cond", par, n)],
             lambda e: e.tensor_tensor(out=condT[:, par, c0:c1], in0=psb[7][:, c0:c1], in1=adab[:, par, c0:c1], op=ALU.add))
        scv = condT[:, par, (3 * n + 1) * 8:(3 * n + 2) * 8]
        gv = pvec[:, par, PV_N1 + 8 * n:PV_N1 + 8 * n + 8]
        gt = condT[:, par, (3 * n + 2) * 8:(3 * n + 3) * 8]
        P.op("dve", [("cond", par, n), ("pvec", par)], [("modA", par, n)],
             lambda e: e.scalar_tensor_tensor(
                 out=modA[:, par, n, :], in0=scv, scalar=1.0, in1=gv, op0=ALU.add, op1=ALU.mult))
        fac = 1.0 if n == 1 else 0.5
        P.op("dve", [("cond", par, n)], [("gate", par, n)],
             lambda e: e.tensor_scalar(out=gate[:, par, n, :], in0=gt, scalar1=fac, scalar2=None, op0=ALU.mult))

    def cond_finish(l):
        for n in range(3):
            cond_finish_n(l, n)

    def ada_emit(al, acc):
        cond_chunk(al, acc)
        if acc in (11, 17):
            cond_flush()
        if al == 0 and acc == 11:
            cond_finish_n(0, 1)
        elif al == 0 and acc == 17:
            cond_finish_n(0, 2)
        elif al > 0 and acc == 17:
            cond_finish(al)

    load_layer_params(0)
    for tt in range(16):
        q = tt % 8
        stage = Gf[:, q * 1024:(q + 1) * 1024]
        P.dma("sp", ("xs", q), [], [tg(q, s) for s in range(4)], [(stage, x_d[tt * 128:(tt + 1) * 128, :])])
        for jg in range(2):
            bank = 2 * (tt % 2) + jg

            def ft(e, stage=stage, jg=jg, bank=bank):
                ins = None
                for jj in range(4):
                    j = 4 * jg + jj
                    ins = e.transpose(psb[bank][:, jj * 128:(jj + 1) * 128], stage[:, j * 128:(j + 1) * 128], ident_f)
                return ins
            P.op("pe", [tg(q, s) for s in range(4)] + ["cst"], [tps(bank)], ft)
            dst = xT[:, 4 * jg:4 * jg + 4, tt * 128:(tt + 1) * 128]
            src = psb[bank][:].rearrange("p (a b) -> p a b", a=4)
            eng = "act" if jg == 0 else "dve"
            if eng == "act":
                P.op("act", [tps(bank)], [tx(4 * jg + jj, tt // 4) for jj in range(4)],
                     lambda e, dst=dst, src=src: e.activation(out=dst, in_=src, func=AF.Copy))
            else:
                P.op("dve", [tps(bank)], [tx(4 * jg + jj, tt // 4) for jj in range(4)],
                     lambda e, dst=dst, src=src: e.tensor_copy(out=dst, in_=src))

    for cc in range(6):
        cond_chunk(0, cc)
    cond_flush()
    cond_finish_n(0, 0)
    RBr.prefetch()

    cnt = {"tA": 0, "tB": 0, "rs": 0}

    def tA_slot():
        i = cnt["tA"]
        cnt["tA"] = (i + 1) % 4
        return i

    def tB_slot():
        i = cnt["tB"]
        cnt["tB"] = (i + 1) % 4
        return i

    def rs_slot():
        i = cnt["rs"]
        cnt["rs"] = (i + 1) % 2
        return i

    def tAv(i, w=512):
        return tA[:, i * 512:i * 512 + w]

    def stats_rstd(src_fn, src_tokens, ntiles, inv_n, w=512):
        r = rs_slot()
        for k in range(ntiles):
            b = tB_slot()
            if ntiles == 8 and k % 2 == 1:
                P.op("dve", [src_tokens[k]], [("tB", b)],
                     lambda e, k=k, b=b: e.tensor_tensor(out=tB[:, b, 0:w], in0=src_fn(k), in1=src_fn(k), op=ALU.mult))
            else:
                P.op("act", [src_tokens[k]], [("tB", b)],
                     lambda e, k=k, b=b: e.activation(out=tB[:, b, 0:w], in_=src_fn(k), func=AF.Square))
            P.op("pe", [("tB", b), "ones_b"], [tps(6)],
                 lambda e, k=k, b=b: e.matmul(psb[6][:, 0:w], ones_b[:], tB[:, b, 0:w], start=(k == 0), stop=(k == ntiles - 1)))
        P.op("act", [tps(6), "epsc"], [("rs", r)],
             lambda e: e.activation(out=rs[:, r, 0:w], in_=psb[6][:, 0:w], func=AF.Ln, bias=epsc[:, 0:1], scale=inv_n))
        P.op("act", [("rs", r)], [("rs", r)],
             lambda e: e.activation(out=rs[:, r, 0:w], in_=rs[:, r, 0:w], func=AF.Exp, scale=-0.5))
        return r

    def norm_p1(l, n, s):
        return stats_rstd(lambda k, s=s: xT[:, k, sl(s)], [tx(k, s) for k in range(8)], 8, 1.0 / D)

    def norm_p2(l, n, s, r):
        par = l % 2
        for k in range(8):
            a = tA_slot()
            P.op("pool", [tx(k, s), ("rs", r)], [("tA", a)],
                 lambda e, k=k, s=s, a=a, r=r: e.tensor_tensor(out=tAv(a), in0=xT[:, k, sl(s)], in1=rs[:, r, :], op=ALU.mult))
            P.op("act", [("tA", a), ("modA", par, n), ("cond", par, n)], [th(k, s)],
                 lambda e, k=k, s=s, a=a: e.activation(
                     out=hT[:, k, sl(s)], in_=tAv(a), func=AF.Identity,
                     bias=condT[:, par, 3 * n * 8 + k:3 * n * 8 + k + 1], scale=modA[:, par, n, k:k + 1]))

    def norm_slice(l, n, s):
        r = norm_p1(l, n, s)
        norm_p2(l, n, s, r)

    def norm_cb(l, n):
        st = {}

        def cb(s):
            st[s] = norm_p1(l, n, s)
            if s >= 1:
                norm_p2(l, n, s - 1, st[s - 1])
            if s == 3:
                norm_p2(l, n, 3, st[3])
        return cb

    def norm_mod(l, n):
        for s in range(4):
            norm_slice(l, n, s)

    def ffn(l, f, after_slice=None, pump=None):
        par = l % 2
        n = 0 if f == 0 else 2
        ada_i = 0
        ra_last = None
        if pump is not None:
            j = RAr.cur
            while j + 1 < len(RAr.chunks) and not RAr.chunks[j + 1].get("barrier"):
                j += 1
            ra_last = j

        def ra_done():
            ci = RAr.cur - 1
            if pump is not None and ci + 3 > ra_last:
                pump.free[ci % 3] = True
        for (m0, mc) in FFN_PARTS:
            lastpart = (m0 + mc == NM)
            for c in range(mc // 2):
                slot = RAr.next(("win", l, f, m0 + 2 * c))
                for s in range(4):
                    for mi in range(2):
                        mloc = 2 * c + mi
                        ba, bb = ab_pair()

                        def fa(e, slot=slot, s=s, col=mi * 128, bank=ba):
                            ins = None
                            for k in range(8):
                                ins = e.matmul(psb[bank][:], RA[:, slot, k, col:col + 128], hT[:, k, sl(s)],
                                               start=(k == 0), stop=(k == 7))
                            return ins
                        P.op("pe", [("RA", slot)] + [th(k, s) for k in range(8)], [tps(ba)], fa)

                        def fb(e, slot=slot, s=s, col=256 + mi * 128, bank=bb):
                            ins = None
                            for k in range(8):
                                ins = e.matmul(psb[bank][:], RA[:, slot, k, col:col + 128], hT[:, k, sl(s)],
                                               start=(k == 0), stop=(k == 7))
                            return ins
                        P.op("pe", [("RA", slot)] + [th(k, s) for k in range(8)], [tps(bb)], fb)
                        a = tA_slot()
                        P.op("act", [tps(ba)], [("tA", a)],
                             lambda e, a=a, ba=ba: e.activation(out=tAv(a), in_=psb[ba][:], func=AF.Silu))
                        P.op("dve", [("tA", a), tps(bb)], [tg(mloc, s)],
                             lambda e, a=a, bb=bb, mloc=mloc, s=s: e.tensor_tensor(
                                 out=G[:, mloc, sl(s)], in0=tAv(a), in1=psb[bb][:], op=ALU.mult))
                        if lastpart and pump is not None:
                            pump.step(4)
                        if s == 0 and mi == 1:
                            cond_flush()
                ra_done()
                for (al, acc) in ada_after(l, f, ada_i):
                    ada_emit(al, acc)
                    ra_done()
                ada_i += 1
            cond_flush()
            if lastpart:
                slots4 = [RBr.next(("wout", l, f, m0, jp), la=(3 if jp == 0 else 0)) for jp in range(4)]
                order = [(jp, s) for s in range(4) for jp in range(4)]
            else:
                slots4 = None
                order = [(jp, s) for jp in range(4) for s in range(4)]
            slot = None
            for (jp, s) in order:
                if lastpart:
                    slot = slots4[jp]
                elif s == 0:
                    slot = RBr.next(("wout", l, f, m0, jp))
                if True:
                    for ji in range(2):
                        j = 2 * jp + ji
                        bank = out_bank()

                        def fo(e, slot=slot, s=s, ji=ji, bank=bank, mc=mc):
                            ins = None
                            for mm in range(mc):
                                ins = e.matmul(psb[bank][:], RB[:, slot, mm, ji * 128:(ji + 1) * 128], G[:, mm, sl(s)],
                                               start=(mm == 0), stop=(mm == mc - 1))
                            return ins
                        P.op("pe", [("RB", slot)] + [tg(mm, s) for mm in range(mc)], [tps(bank)], fo)
                        P.op("dve", [tps(bank), tx(j, s), ("gate", par, n)], [tx(j, s)],
                             lambda e, bank=bank, j=j, s=s: e.scalar_tensor_tensor(
                                 out=xT[:, j, sl(s)], in0=psb[bank][:], scalar=gate[:, par, n, j:j + 1],
                                 in1=xT[:, j, sl(s)], op0=ALU.mult, op1=ALU.add))
                if lastpart and pump is not None:
                    pump.step(4)
                if lastpart and jp == 3 and after_slice is not None:
                    after_slice(s)
            if lastpart and pump is not None:
                pump.flush()
            if lastpart:
                RBr.prefetch()

    def gnorm(l, q, srcs, src_tokens, s, w):
        par = l % 2
        r = stats_rstd(lambda k: srcs[k], src_tokens, 2, 1.0 / 256, w=w)
        for ct in range(2):
            P.op("dve", [src_tokens[ct], ("rs", r), ("pvec", par)], [tg(2 * q + ct, (s * w) // 512)],
                 lambda e, ct=ct: e.scalar_tensor_tensor(
                     out=G[:, 2 * q + ct, s * w:(s + 1) * w], in0=srcs[ct],
                     scalar=pvec[:, par, PV_MNG + 2 * q + ct:PV_MNG + 2 * q + ct + 1],
                     in1=rs[:, r, 0:w], op0=ALU.mult, op1=ALU.mult))

    def mm_cols(slot, c0, rhs_fn, bank_ap, reads, wtok):
        def fm(e):
            ins = None
            for k in range(8):
                ins = e.matmul(bank_ap, RB[:, slot, k, c0:c0 + 128], rhs_fn(k), start=(k == 0), stop=(k == 7))
            return ins
        P.op("pe", [("RB", slot)] + reads, [wtok], fm)

    def s5_setup_early(l):
        lists = {k: [] for k in ("free", "fold", "tab0", "tab1", "mat", "c256")}
        cur = {"name": "free", "gate": None}
        Q = lambda *a: lists[cur["name"]].append((cur["gate"], lambda a=a: P.op(*a)))
        QD = lambda *a: lists[cur["name"]].append((cur["gate"], lambda a=a: P.dma(*a)))
        par = l % 2
        pv = pvec[:, par, :]
        lre = pv[:, PV_LRE:PV_LRE + 8]
        lim = pv[:, PV_LIM:PV_LIM + 8]
        ldt = pv[:, PV_LDT:PV_LDT + 8]
        T = lambda i: s5s[:, i, :]
        tok = ("s5s",)
        rd = [("pvec", par), tok]
        dv = lambda fn: Q("dve", rd, [tok], fn)
        ac = lambda fn: Q("act", rd, [tok], fn)
        cur.update(name="mat", gate=2)
        QD("pool", ("s5c", 0), [], [("RA", 2)], [(CTb[0], cp_d[l, 0]), (CTb[1], cp_d[l, 1])])
        Q("act", [("RA", 2)], ["nCTim"], lambda e: e.activation(out=nCTim[:], in_=CTb[1], func=AF.Identity, scale=-1.0))
        cur.update(name="free", gate=None)
        QD("pool", ("s5c", 1), [], ["gluw"], [(gluw[:], gluw_d[l].rearrange("(k p) n -> p k n", p=128))])
        QD("pool", ("s5c", 2), [], ["poolw"], [(poolw[:], pool_d[l])])
        QD("sp", ("s5b", 1), [], ["bsT"], [(bsT[:], sgub_d[l])])
        ac(lambda e: e.activation(out=T(0), in_=ldt, func=AF.Exp))
        dv(lambda e: e.tensor_tensor(out=T(1), in0=lim, in1=T(0), op=ALU.mult))
        dv(lambda e: e.tensor_tensor(out=T(10), in0=lre, in1=T(0), op=ALU.mult))
        ac(lambda e: e.activation(out=T(2), in_=T(10), func=AF.Exp))
        dv(lambda e: e.tensor_scalar(out=T(10), in0=T(1), scalar1=1.0 / TWO_PI, scalar2=MAGIC, op0=ALU.mult, op1=ALU.add))
        dv(lambda e: e.tensor_scalar(out=T(10), in0=T(10), scalar1=MAGIC, scalar2=None, op0=ALU.subtract))
        dv(lambda e: e.scalar_tensor_tensor(out=T(1), in0=T(10), scalar=-TWO_PI, in1=T(1), op0=ALU.mult, op1=ALU.add))
        dv(lambda e: e.tensor_scalar(out=T(1), in0=T(1), scalar1=math.pi, scalar2=-math.pi, op0=ALU.min, op1=ALU.max))
        ac(lambda e: e.activation(out=T(3), in_=T(1), func=AF.Sin))
        dv(lambda e: e.tensor_scalar(out=T(10), in0=T(1), scalar1=1.0 / TWO_PI, scalar2=0.25, op0=ALU.mult, op1=ALU.add))
        dv(lambda e: e.tensor_scalar(out=T(11), in0=T(10), scalar1=MAGIC, scalar2=MAGIC, op0=ALU.add, op1=ALU.subtract))
        dv(lambda e: e.tensor_tensor(out=T(10), in0=T(10), in1=T(11), op=ALU.subtract))
        dv(lambda e: e.tensor_scalar(out=T(10), in0=T(10), scalar1=TWO_PI, scalar2=None, op0=ALU.mult))
        dv(lambda e: e.tensor_scalar(out=T(10), in0=T(10), scalar1=math.pi, scalar2=-math.pi, op0=ALU.min, op1=ALU.max))
        ac(lambda e: e.activation(out=T(4), in_=T(10), func=AF.Sin))
        dv(lambda e: e.tensor_tensor(out=T(5), in0=T(2), in1=T(4), op=ALU.mult))
        dv(lambda e: e.tensor_tensor(out=T(6), in0=T(2), in1=T(3), op=ALU.mult))
        dv(lambda e: e.tensor_tensor(out=T(7), in0=lre, in1=lre, op=ALU.mult))
        dv(lambda e: e.tensor_tensor(out=T(10), in0=lim, in1=lim, op=ALU.mult))
        dv(lambda e: e.tensor_tensor(out=T(7), in0=T(7), in1=T(10), op=ALU.add))
        dv(lambda e: e.reciprocal(out=T(7), in_=T(7)))
        dv(lambda e: e.tensor_scalar(out=T(11), in0=T(5), scalar1=-1.0, scalar2=None, op0=ALU.add))
        dv(lambda e: e.tensor_tensor(out=T(8), in0=T(11), in1=lre, op=ALU.mult))
        dv(lambda e: e.tensor_tensor(out=T(10), in0=T(6), in1=lim, op=ALU.mult))
        dv(lambda e: e.tensor_tensor(out=T(8), in0=T(8), in1=T(10), op=ALU.add))
        dv(lambda e: e.tensor_tensor(out=T(8), in0=T(8), in1=T(7), op=ALU.mult))
        dv(lambda e: e.tensor_tensor(out=T(9), in0=T(6), in1=lre, op=ALU.mult))
        dv(lambda e: e.tensor_tensor(out=T(10), in0=T(11), in1=lim, op=ALU.mult))
        dv(lambda e: e.tensor_tensor(out=T(9), in0=T(9), in1=T(10), op=ALU.subtract))
        dv(lambda e: e.tensor_tensor(out=T(9), in0=T(9), in1=T(7), op=ALU.mult))
        SIN_SC = TWO_PI * (1.0 - 1e-6)
        scn = g5rs[:]
        for i in range(8):
            hf = i // 4
            cur.update(name="tab%d" % hf, gate=hf)
            th_i = s5s[:, 1, i:i + 1]
            rtok = [("RA", hf)]
            ts_, tc_ = tab[:, i, 1, :], tab[:, i, 0, :]
            Q("dve", [tok, "cst"], rtok,
              lambda e, th_i=th_i, ts_=ts_: e.tensor_scalar(out=ts_, in0=iota, scalar1=th_i, scalar2=1.0 / TWO_PI, op0=ALU.mult, op1=ALU.mult))
            Q("dve", rtok, [("g5rs",)],
              lambda e, ts_=ts_: e.tensor_scalar(out=scn, in0=ts_, scalar1=MAGIC, scalar2=MAGIC, op0=ALU.add, op1=ALU.subtract))
            Q("dve", rtok + [("g5rs",)], rtok,
              lambda e, ts_=ts_: e.tensor_tensor(out=ts_, in0=ts_, in1=scn, op=ALU.subtract))
            Q("dve", rtok, rtok,
              lambda e, ts_=ts_, tc_=tc_: e.tensor_scalar(out=tc_, in0=ts_, scalar1=0.25, scalar2=None, op0=ALU.add))
            Q("dve", rtok, [("g5rs",)],
              lambda e, tc_=tc_: e.tensor_scalar(out=scn, in0=tc_, scalar1=MAGIC, scalar2=MAGIC, op0=ALU.add, op1=ALU.subtract))
            Q("dve", rtok + [("g5rs",)], rtok,
              lambda e, tc_=tc_: e.tensor_tensor(out=tc_, in0=tc_, in1=scn, op=ALU.subtract))
            if i % 4 == 3:
                tv = RAf[:, hf * 2048:(hf + 1) * 2048]
                Q("act", [("RA", hf)], [("RA", hf)], lambda e, tv=tv: e.activation(out=tv, in_=tv, func=AF.Sin, scale=SIN_SC))
        cur.update(name="c256", gate=None)
        c255 = tab[:, :, 0, 255]
        s255 = tab[:, :, 1, 255]
        rd2 = [tok, ("RA", 0), ("RA", 1)]
        Q("dve", rd2, [tok], lambda e: e.tensor_tensor(out=T(12), in0=c255, in1=T(4), op=ALU.mult))
        Q("dve", rd2, [tok], lambda e: e.tensor_tensor(out=T(10), in0=s255, in1=T(3), op=ALU.mult))
        Q("dve", rd2, [tok], lambda e: e.tensor_tensor(out=T(12), in0=T(12), in1=T(10), op=ALU.subtract))
        Q("dve", rd2, [tok], lambda e: e.tensor_tensor(out=T(13), in0=s255, in1=T(4), op=ALU.mult))
        Q("dve", rd2, [tok], lambda e: e.tensor_tensor(out=T(10), in0=c255, in1=T(3), op=ALU.mult))
        Q("dve", rd2, [tok], lambda e: e.tensor_tensor(out=T(13), in0=T(13), in1=T(10), op=ALU.add))
        Q("dve", rd2, [tok], lambda e: e.tensor_scalar(out=T(14), in0=T(13), scalar1=-1.0, scalar2=None, op0=ALU.mult))
        cur.update(name="free", gate=None)
        Q("dve", [], [("s5i", i) for i in range(8)], lambda e: e.memset(s5i[:], 0.0))
        for ct in range(2):
            Q("act", ["cst", ("pvec", par)], ["diagD"], lambda e, ct=ct: e.activation(
                out=diagD[:, ct, :], in_=ident_f, func=AF.Identity, scale=pvec[:, par, PV_D + ct:PV_D + ct + 1]))
        cur.update(name="fold", gate=None)
        tCf = tC[:].rearrange("p a b -> p (a b)")
        t4 = [("tC", j) for j in range(4)]
        for hf in range(2):
            Bp = [tCf[:, 0:512].rearrange("p (i c) -> p i c", i=4), tCf[:, 512:1024].rearrange("p (i c) -> p i c", i=4)]
            Bb = [tCf[:, 1024:1536].rearrange("p (i c) -> p i c", i=4), tCf[:, 1536:2048].rearrange("p (i c) -> p i c", i=4)]
            QD("sp", ("s5b", 0), [], t4, [(Bp[0], bp_d[l, 0][:, 4 * hf:4 * hf + 4, :]), (Bp[1], bp_d[l, 1][:, 4 * hf:4 * hf + 4, :])])
            for ii in range(4):
                i = 4 * hf + ii
                kre = s5s[:, 8, i:i + 1]
                kim = s5s[:, 9, i:i + 1]
                Q("dve", t4 + [tok], t4, lambda e, ii=ii, kim=kim, Bp=Bp, Bb=Bb: e.tensor_scalar(
                    out=Bb[0][:, ii, :], in0=Bp[1][:, ii, :], scalar1=kim, scalar2=None, op0=ALU.mult))
                Q("dve", t4 + [tok], t4, lambda e, ii=ii, kre=kre, Bp=Bp, Bb=Bb: e.scalar_tensor_tensor(
                    out=Bb[0][:, ii, :], in0=Bp[0][:, ii, :], scalar=kre, in1=Bb[0][:, ii, :], op0=ALU.mult, op1=ALU.subtract))
                Q("dve", t4 + [tok], t4, lambda e, ii=ii, kim=kim, Bp=Bp, Bb=Bb: e.tensor_scalar(
                    out=Bb[1][:, ii, :], in0=Bp[0][:, ii, :], scalar1=kim, scalar2=None, op0=ALU.mult))
                Q("dve", t4 + [tok], t4, lambda e, ii=ii, kre=kre, Bp=Bp, Bb=Bb: e.scalar_tensor_tensor(
                    out=Bb[1][:, ii, :], in0=Bp[1][:, ii, :], scalar=kre, in1=Bb[1][:, ii, :], op0=ALU.mult, op1=ALU.add))
            for ri in range(2):
                bank = 6
                cur.update(gate=2)

                def ftr(e, ri=ri, bank=bank, Bb=Bb):
                    ins = None
                    for ii in range(4):
                        ins = e.transpose(psb[bank][:, ii * 128:(ii + 1) * 128], Bb[ri][:, ii, :], ident_f)
                    return ins
                Q("pe", t4 + ["cst"], [tps(bank)], ftr)
                Q("act", [tps(bank)], [("RA", 2)], lambda e, ri=ri, hf=hf, bank=bank: e.activation(
                    out=BbT[ri][:, 4 * hf:4 * hf + 4, :], in_=psb[bank][:].rearrange("p (a b) -> p a b", a=4), func=AF.Copy))
            cur.update(gate=None)
        QD("sp", ("s5b", 2), [], t4, [(tC[:, h, 0:128], sguw_d[l, h]) for h in range(4)])

        def ftw(e):
            ins = None
            for h in range(4):
                ins = e.transpose(psb[6][:, h * 128:(h + 1) * 128], tC[:, h, 0:128], ident_f)
            return ins
        Q("pe", t4 + ["cst"], [tps(6)], ftw)
        Q("dve", [tps(6), "cst"], ["WsT"], lambda e: e.tensor_tensor(
            out=WsT[:], in0=psb[6][:].rearrange("p (a b) -> p a b", a=4),
            in1=trimask.unsqueeze(1).to_broadcast([128, 4, 128]), op=ALU.mult))
        return lists

    class Pump:
        ORDER = ("free", "fold", "mat", "tab0", "tab1", "c256")

        def __init__(self, lists):
            self.l = lists
            self.free = {0: False, 1: False, 2: False}

        def step(self, n):
            for _ in range(n):
                done = True
                for nm in self.ORDER:
                    q = self.l[nm]
                    if not q:
                        continue
                    if nm == "c256" and (self.l["tab0"] or self.l["tab1"]):
                        continue
                    gate, th_ = q[0]
                    if gate is not None and not self.free[gate]:
                        continue
                    q.pop(0)
                    th_()
                    done = False
                    break
                if done:
                    return

        def flush(self):
            self.free = {0: True, 1: True, 2: True}
            while any(self.l[k] for k in self.ORDER):
                self.step(1000)

    def tap(nm, src_ap, reads):
        if nm in taps_d:
            P.dma("pool", ("tap", nm), reads, [], [(taps_d[nm], src_ap)])

    def mixer_s5(l):
        par = l % 2
        slot = RBr.next(("mi", l, "D"))
        W = 256
        tok = ("s5s",)
        RT = [("RA", 0), ("RA", 1)]
        bc = lambda ap: ap.unsqueeze(1).to_broadcast([128, 2, W])
        ybank = lambda s8: 5 if s8 % 2 == 0 else 6
        tAb = tA[:].bitcast(BF16)
        tBf = tB[:].rearrange("p a b -> p (a b)")
        ubv = tC[:, 2, :].bitcast(BF16)
        ygv = tC[:, 3, :].bitcast(BF16)
        ub = lambda ct: ubv[:, ct * 256:ct * 256 + 256]
        ygb = lambda ct: ygv[:, ct * 256:ct * 256 + 256]

        def wbuf(k):
            return tA[:, k * 512:k * 512 + 256], tA[:, k * 512 + 256:k * 512 + 512], [("tA8", 2 * k), ("tA8", 2 * k + 1)]

        def abuf(k):
            o = 2048 + k * 1024
            return tAb[:, o:o + 512], tAb[:, o + 512:o + 1024], [("tA8", 4 + 2 * k), ("tA8", 5 + 2 * k)]

        rsb = rs[:].rearrange("p a b -> p (a b)").bitcast(BF16)

        def pbuf(k):
            if k < 2:
                o = k * 1024
                return tBf[:, o:o + 512], tBf[:, o + 512:o + 1024], [("tB", 2 * k), ("tB", 2 * k + 1)]
            return rsb[:, 0:512], rsb[:, 512:1024], [("rs", 0), ("rs", 0)]

        def prologue(s8):
            t0 = s8 * W
            hs = s8 // 2
            for ct in range(2):
                mm_cols(slot, ct * 128, lambda k, t0=t0: hT[:, k, t0:t0 + W], psb[4][:, ct * 256:ct * 256 + 256],
                        [th(k, hs) for k in range(8)], tps(4))
            P.op("act", [tps(4)], [("tC", 2)], lambda e: e.activation(out=ubv[:, 0:512], in_=psb[4][:], func=AF.Copy))

        def ystart(s8, ct):
            yb = ybank(s8)
            P.op("pe", ["diagD", ("tC", 2)], [tps(yb)], lambda e, ct=ct, yb=yb: e.matmul(
                psb[yb][:, ct * 256:ct * 256 + 256], diagD[:, ct, :], ub(ct), start=True, stop=False))

        def SA(s8, i):
            ct = i // 4
            k = i % 2
            bub = [psb[k][:, 0:256], psb[k][:, 256:512]]
            for ri in range(2):
                P.op("pe", [("RA", 2), ("tC", 2)], [tps(k)], lambda e, ri=ri, i=i, ct=ct, bub=bub: e.matmul(
                    bub[ri], BbT[ri][:, i, :], ub(ct), start=True, stop=True))
            tabi = tab[:, i]
            A_, Bs_, atok = abuf(k)
            P.op("dve", [tps(k)] + RT, [atok[0]], lambda e, tabi=tabi, bub=bub, A_=A_: e.tensor_tensor(
                out=A_.rearrange("p (c t) -> p c t", c=2), in0=tabi, in1=bc(bub[0]), op=ALU.mult))
            P.op("dve", [tps(k)] + RT, [atok[1]], lambda e, tabi=tabi, bub=bub, Bs_=Bs_: e.tensor_tensor(
                out=Bs_[:, 0:256], in0=tabi[:, 1, :], in1=bub[1], op=ALU.mult))
            P.op("dve", [tps(k)] + RT, [atok[1]], lambda e, tabi=tabi, bub=bub, Bs_=Bs_: e.scalar_tensor_tensor(
                out=Bs_[:, 256:512], in0=tabi[:, 0, :], scalar=-1.0, in1=bub[1], op0=ALU.mult, op1=ALU.mult))

        def SB(s8, i):
            k = i % 2
            A_, Bs_, atok = abuf(k)
            vb = 2 + k

            def fadd(e, A_=A_, Bs_=Bs_, vb=vb):
                e.matmul(psb[vb][:], ident_b[:], A_, start=True, stop=False)
                return e.matmul(psb[vb][:], ident_b[:], Bs_, start=False, stop=True)
            P.op("pe", atok + ["ident_b"], [tps(vb)], fadd)
            wre_b, wn_b, wtok = wbuf(k)
            rbc = s5s[:, 2, i:i + 1].to_broadcast([128, W])
            P.op("dve", [tps(vb), tok, ("s5i", i)], [wtok[0]], lambda e, rbc=rbc, i=i, wre_b=wre_b, vb=vb: e.tensor_tensor_scan(
                out=wre_b, data0=rbc, data1=psb[vb][:, 0:256], initial=s5i[:, 0, i:i + 1], op0=ALU.mult, op1=ALU.add))
            P.op("dve", [tps(vb), tok, ("s5i", i)], [wtok[1]], lambda e, rbc=rbc, i=i, wn_b=wn_b, vb=vb: e.tensor_tensor_scan(
                out=wn_b, data0=rbc, data1=psb[vb][:, 256:512], initial=s5i[:, 1, i:i + 1], op0=ALU.mult, op1=ALU.add))
            wre = wre_b[:, W - 1:W]
            wn = wn_b[:, W - 1:W]
            c256 = s5s[:, 12, i:i + 1]
            s256 = s5s[:, 13, i:i + 1]
            ns256 = s5s[:, 14, i:i + 1]
            ta_ = smal[:, 32:33]
            tb_ = smal[:, 33:34]
            P.op("act", [wtok[1], tok], [("smal",)], lambda e, wn=wn, s256=s256: e.activation(
                out=ta_, in_=wn, func=AF.Identity, scale=s256))
            P.op("act", [wtok[0], tok, ("smal",)], [("s5i", i)], lambda e, wre=wre, c256=c256, i=i: e.activation(
                out=s5i[:, 0, i:i + 1], in_=wre, func=AF.Identity, scale=c256, bias=ta_))
            P.op("act", [wtok[1], tok], [("smal",)], lambda e, wn=wn, c256=c256: e.activation(
                out=tb_, in_=wn, func=AF.Identity, scale=c256))
            P.op("act", [wtok[0], tok, ("smal",)], [("s5i", i)], lambda e, wre=wre, ns256=ns256, i=i: e.activation(
                out=s5i[:, 1, i:i + 1], in_=wre, func=AF.Identity, scale=ns256, bias=tb_))
            tabi = tab[:, i]
            Ap_, Bp_, ptok = pbuf((s8 * 8 + i) % 3)
            P.op("pool", [wtok[0]] + RT, [ptok[0]], lambda e, tabi=tabi, wre_b=wre_b, Ap_=Ap_: e.tensor_tensor(
                out=Ap_.rearrange("p (c t) -> p c t", c=2), in0=tabi, in1=bc(wre_b), op=ALU.mult))
            P.op("pool", [wtok[1]] + RT, [ptok[1]], lambda e, tabi=tabi, wn_b=wn_b, Bp_=Bp_: e.tensor_tensor(
                out=Bp_.rearrange("p (c t) -> p c t", c=2), in0=tabi, in1=bc(wn_b), op=ALU.mult))

        def SC(s8, i):
            ct = i // 4
            k = i % 2
            if i == 4:
                ystart(s8, 1)
            yb = ybank(s8)
            yap = psb[yb][:, ct * 256:ct * 256 + 256]
            Ap_, Bp_, ptok = pbuf((s8 * 8 + i) % 3)
            last = (i % 4 == 3)

            def fc(e, i=i, yap=yap, Ap_=Ap_, Bp_=Bp_, last=last):
                e.matmul(yap, CTb[0][:, i, :], Ap_[:, 0:256], start=False, stop=False)
                e.matmul(yap, CTb[0][:, i, :], Bp_[:, 256:512], start=False, stop=False)
                e.matmul(yap, nCTim[:, i, :], Ap_[:, 256:512], start=False, stop=False)
                return e.matmul(yap, CTb[1][:, i, :], Bp_[:, 0:256], start=False, stop=last)
            P.op("pe", [("RA", 2), "nCTim"] + ptok, [tps(yb)], fc)

        def epilogue(s8, part):
            yb = ybank(s8)
            yg = [tC[:, 0, 0:W], tC[:, 1, 0:W]]
            yd = [tC[:, 0, 256:256 + W], tC[:, 1, 256:256 + W]]
            if part == 0:
              for ct in range(2):
                P.op("act", [tps(yb)], [("tC", ct)], lambda e, ct=ct: e.activation(
                    out=yg[ct], in_=psb[yb][:, ct * 256:ct * 256 + 256], func=AF.Gelu_apprx_tanh))
                P.op("act", [("tC", ct)], [("tC", 3)], lambda e, ct=ct: e.activation(out=ygb(ct), in_=yg[ct], func=AF.Copy))
              if l == 0 and s8 == 0:
                tap("s5y", tC[:, 0, 0:W], [("tC", 0)])
              return
            if part == 1:
              for co in range(2):
                gap = psb[4][:, co * 256:co * 256 + 256]

                def fg(e, co=co, gap=gap):
                    ins = None
                    for ct in range(2):
                        ins = e.matmul(gap, gluw[:, ct, co * 128:(co + 1) * 128], ygb(ct), start=(ct == 0), stop=(ct == 1))
                    return ins
                P.op("pe", ["gluw", ("tC", 3)], [tps(4)], fg)
              return
            if part == 2:
              for co in range(2):
                gap = psb[4][:, co * 256:co * 256 + 256]
                P.op("act", [tps(4), ("pvec", par)], [("tC", co)], lambda e, co=co, gap=gap: e.activation(
                    out=yd[co], in_=gap, func=AF.Sigmoid, bias=pvec[:, par, PV_GB + co:PV_GB + co + 1], scale=1.0))
              return
            if part == 3:
              for co in range(2):
                P.op("dve", [("tC", co)], [("tC", co)], lambda e, co=co: e.tensor_tensor(
                    out=yd[co], in0=yd[co], in1=yg[co], op=ALU.mult))
              for ct in range(2):
                P.op("act", [("tC", ct)], [("g5sq", ct)], lambda e, ct=ct: e.activation(out=g5sq[:, ct, :], in_=yd[ct], func=AF.Square))
              return
            if part == 4:
              for ct in range(2):
                P.op("pe", [("g5sq", ct), "ones_b"], [tps(4)], lambda e, ct=ct: e.matmul(
                    psb[4][:, 0:W], ones_b[:], g5sq[:, ct, :], start=(ct == 0), stop=(ct == 1)))
              P.op("act", [tps(4), "epsc"], [("g5rs",)], lambda e: e.activation(
                out=g5rs[:], in_=psb[4][:, 0:W], func=AF.Ln, bias=epsc[:, 0:1], scale=1.0 / 256))
              P.op("act", [("g5rs",)], [("g5rs",)], lambda e: e.activation(out=g5rs[:], in_=g5rs[:], func=AF.Exp, scale=-0.5))
              return
            for ct in range(2):
                P.op("dve", [("tC", ct), ("g5rs",), ("pvec", par)], [tg(6 + ct, (s8 * W) // 512)], lambda e, ct=ct: e.scalar_tensor_tensor(
                    out=G[:, 6 + ct, s8 * W:(s8 + 1) * W], in0=yd[ct], scalar=pvec[:, par, PV_MNG + 6 + ct:PV_MNG + 6 + ct + 1],
                    in1=g5rs[:], op0=ALU.mult, op1=ALU.mult))

        prologue(0)
        NG = 64
        ep_sched = {}
        EP_OFF = [11, 13, 14, 15, 16, 18]
        for s8 in range(8):
            for part in range(6):
                ep_sched.setdefault(s8 * 8 + EP_OFF[part], []).append((s8, part))
        for g in range(NG + 20):
            if g < NG:
                s8, i = divmod(g, 8)
                if i == 3:
                    pass
                SA(s8, i)
            if 0 <= g - 1 < NG:
                s8, i = divmod(g - 1, 8)
                SB(s8, i)
            if 0 <= g - 3 < NG:
                s8, i = divmod(g - 3, 8)
                if i == 0:
                    ystart(s8, 0)
                SC(s8, i)
            if g < NG and g % 8 == 7 and g + 1 < NG:
                prologue(g // 8 + 1)
            for (s8, part) in ep_sched.get(g, []):
                epilogue(s8, part)

    def mixer_sgu(l):
        par = l % 2
        slot_u = RBr.next(("mi", l, "Au"))
        slot_v = RBr.next(("mi", l, "Av"), la=0)
        for s in range(4):
            ug = [tAv(0), tAv(1)]
            for ct in range(2):
                mm_cols(slot_u, ct * 128, lambda k, s=s: hT[:, k, sl(s)], psb[ct][:], [th(k, s) for k in range(8)], tps(ct))
                P.op("act", [tps(ct)], [("tA", ct)], lambda e, ct=ct: e.activation(out=ug[ct], in_=psb[ct][:], func=AF.Gelu_apprx_tanh))
            v3 = lambda ap: ap.rearrange("p (h d) -> p h d", h=4)
            cen_of = lambda nl: tC[:, 1 + nl // 2, (nl % 2) * 256:(nl % 2) * 256 + 256]
            cen_tok = lambda nl: ("tC", 1 + nl // 2)
            smt = ("smal", 0)
            for nl in range(4):
                n = 4 * s + nl
                vb = 2 + nl // 2
                zv = psb[vb][:, (nl % 2) * 256:(nl % 2) * 256 + 256]
                vg = tC[:, 0, (nl % 2) * 256:(nl % 2) * 256 + 256]
                cen = cen_of(nl)
                sq = tC[:, 3, 0:256]

                def fv(e, n=n, zv=zv):
                    ins = None
                    for kk in range(8):
                        ins = e.matmul(zv, hT[:, kk, n * 128:(n + 1) * 128], RB[:, slot_v, kk, :], start=(kk == 0), stop=(kk == 7))
                    return ins
                P.op("pe", [("RB", slot_v)] + [th(kk, s) for kk in range(8)], [tps(vb)], fv)
                P.op("act", [tps(vb)], [("tC", 0)], lambda e, zv=zv, vg=vg: e.activation(out=vg, in_=zv, func=AF.Gelu_apprx_tanh))
                P.op("dve", [("tC", 0)], [smt], lambda e, vg=vg, nl=nl: e.tensor_reduce(
                    out=smal[:, 4 * nl:4 * nl + 4], in_=v3(vg), axis=AX.X, op=ALU.add))
                P.op("dve", [("tC", 0), smt], [cen_tok(nl)], lambda e, vg=vg, cen=cen, nl=nl: e.scalar_tensor_tensor(
                    out=v3(cen), in0=smal[:, 4 * nl:4 * nl + 4].unsqueeze(2).to_broadcast([128, 4, 64]), scalar=-1.0 / 64,
                    in1=v3(vg), op0=ALU.mult, op1=ALU.add))
                P.op("dve", [cen_tok(nl)], [("tC", 3)], lambda e, cen=cen, sq=sq: e.tensor_tensor(out=sq, in0=cen, in1=cen, op=ALU.mult))
                P.op("dve", [("tC", 3)], [("smal", 1)], lambda e, sq=sq, nl=nl: e.tensor_reduce(
                    out=smal[:, 16 + 4 * nl:16 + 4 * nl + 4], in_=v3(sq), axis=AX.X, op=ALU.add))
                if l == 0 and n == 0:
                    tap("sguvn", cen, [cen_tok(nl)])
            P.op("act", [("smal", 1), "epsc"], [("smal", 1)], lambda e: e.activation(
                out=smal[:, 16:32], in_=smal[:, 16:32], func=AF.Ln, bias=epsc[:, 0:1], scale=1.0 / 64))
            P.op("act", [("smal", 1)], [("smal", 1)], lambda e: e.activation(
                out=smal[:, 16:32], in_=smal[:, 16:32], func=AF.Exp, scale=-0.5))
            for nl in range(4):
                cen = cen_of(nl)
                vn = tB_slot()
                P.op("dve", [cen_tok(nl), ("smal", 1)], [("tB", vn)], lambda e, vn=vn, cen=cen, nl=nl: e.tensor_tensor(
                    out=v3(tB[:, vn, 0:256]), in0=v3(cen),
                    in1=smal[:, 16 + 4 * nl:16 + 4 * nl + 4].unsqueeze(2).to_broadcast([128, 4, 64]), op=ALU.mult))

                def fm(e, vn=vn, nl=nl):
                    ins = None
                    for h in range(4):
                        ct = h // 2
                        po = (h % 2) * 64
                        ins = e.matmul(psb[4 + ct][po:po + 64, nl * 128:(nl + 1) * 128], tB[:, vn, h * 64:(h + 1) * 64],
                                       WsT[:, h, :], start=True, stop=True)
                    return ins
                P.op("pe", [("tB", vn), "WsT"], [tps(4), tps(5)], fm)
            ya = [tC[:, 2, 0:512], tC[:, 3, 0:512]]
            for ct in range(2):
                P.op("dve", [tps(4 + ct), "bsT"], [("tC", 2 + ct)], lambda e, ct=ct: e.tensor_tensor(
                    out=ya[ct].rearrange("p (a b) -> p a b", a=4), in0=psb[4 + ct][:].rearrange("p (a b) -> p a b", a=4),
                    in1=bsT[:, ct, :].unsqueeze(1).to_broadcast([128, 4, 128]), op=ALU.add))
                P.op("dve", [("tC", 2 + ct), ("tA", ct)], [("tC", 2 + ct)], lambda e, ct=ct: e.tensor_tensor(
                    out=ya[ct], in0=ya[ct], in1=ug[ct], op=ALU.mult))
            if l == 0 and s == 0:
                tap("sguya", ya[0], [("tC", 2)])
            gnorm(l, 0, ya, [("tC", 2), ("tC", 3)], s, 512)
        RBr.prefetch()

    def mixer_pool(l):
        par = l % 2
        slot = RBr.next(("mi", l, "B"))
        zb = [tC[:, 0, :], tC[:, 1, :]]
        pa = tA[:, 0:528]
        pb = tA[:, 528:1056]
        pc = tA[:, 1056:1584]
        ptoks = [("tA", 0), ("tA", 1), ("tA", 2), ("tA", 3)]
        invw = cst[:, C_IW:C_IW + 2]
        icnt = cst[:, C_ICNT:C_ICNT + 32].rearrange("p (a b) -> p a b", a=2)
        for ct in range(2):
            P.op("dve", [], [("tC", ct)], lambda e, ct=ct: e.memset(zb[ct][:, 0:16], 0.0))
        for s in range(4):
            yb = [tC[:, 2, 0:512], tC[:, 3, 0:512]]
            for ct in range(2):
                mm_cols(slot, ct * 128, lambda k, s=s: hT[:, k, sl(s)], psb[ct][:], [th(k, s) for k in range(8)], tps(ct))
                P.op("act", [tps(ct)], [("tC", ct)], lambda e, ct=ct: e.activation(out=zb[ct][:, 16:528], in_=psb[ct][:], func=AF.Copy))
                z = zb[ct]
                P.op("dve", [("tC", ct)], ptoks, lambda e, z=z: e.tensor_tensor(out=pa[:, 2:528], in0=z[:, 2:528], in1=z[:, 1:527], op=ALU.add))
                P.op("dve", ptoks, ptoks, lambda e: e.tensor_tensor(out=pb[:, 4:528], in0=pa[:, 4:528], in1=pa[:, 2:526], op=ALU.add))
                if ct == 0:
                    P.op("dve", ptoks, ptoks, lambda e: e.tensor_copy(out=pb[0:64, 16:528], in_=pa[0:64, 16:528]))
                    ws = pb
                else:
                    P.op("dve", ptoks, ptoks, lambda e: e.tensor_tensor(out=pc[:, 8:528], in0=pb[:, 8:528], in1=pb[:, 4:524], op=ALU.add))
                    P.op("dve", ptoks, ptoks, lambda e: e.tensor_tensor(out=pa[64:128, 16:528], in0=pc[64:128, 16:528], in1=pc[64:128, 8:520], op=ALU.add))
                    P.op("dve", ptoks, ptoks, lambda e: e.tensor_copy(out=pa[0:64, 16:528], in_=pc[0:64, 16:528]))
                    ws = pa
                pbf = tB_slot()
                P.op("dve", ptoks + [("tC", ct), "cst"], [("tB", pbf)], lambda e, ws=ws, z=z, ct=ct, pbf=pbf: e.scalar_tensor_tensor(
                    out=tB[:, pbf, :], in0=ws[:, 16:528], scalar=invw[:, ct:ct + 1], in1=z[:, 16:528], op0=ALU.mult, op1=ALU.subtract))
                if s == 0:
                    P.op("dve", ptoks + ["cst"], ptoks, lambda e, ws=ws, ct=ct: e.tensor_tensor(
                        out=ws[:, 16:32], in0=ws[:, 16:32], in1=icnt[:, ct, :], op=ALU.mult))
                    P.op("dve", ptoks + [("tC", ct)], [("tB", pbf)], lambda e, ws=ws, z=z, pbf=pbf: e.tensor_tensor(
                        out=tB[:, pbf, 0:16], in0=ws[:, 16:32], in1=z[:, 16:32], op=ALU.subtract))
                P.op("dve", [("tC", ct)], [("tC", ct)], lambda e, z=z: e.tensor_copy(out=z[:, 0:16], in_=z[:, 512:528]))
                bank = 4 + ct
                P.op("pe", ["poolw", ("tB", pbf)], [tps(bank)], lambda e, ct=ct, pbf=pbf, bank=bank: e.matmul(
                    psb[bank][:], poolw[:, ct, :], tB[:, pbf, :], start=True, stop=True))
                P.op("act", [tps(bank), ("pvec", par)], [("tC", 2 + ct)], lambda e, ct=ct, bank=bank: e.activation(
                    out=yb[ct], in_=psb[bank][:], func=AF.Identity, scale=pvec[:, par, PV_PSC + ct:PV_PSC + ct + 1]))
            if l == 0 and s == 0:
                tap("poolyb", yb[0], [("tC", 2)])
            gnorm(l, 1, yb, [("tC", 2), ("tC", 3)], s, 512)

    def mixer_conv(l):
        par = l % 2
        slot_c = RBr.next(("mi", l, "Cc"))
        slot_x = RBr.next(("mi", l, "Cx"), la=0)
        slot_b = RBr.next(("mi", l, "Cb"), la=0)
        yb_ = [tC[:, 0, 0:514], tC[:, 1, 0:514]]
        for ct in range(2):
            P.op("dve", [], [("tC", ct)], lambda e, ct=ct: e.memset(yb_[ct][:, 0:2], 0.0))
        for s in range(4):
            yc = [tC[:, 2, 0:512], tC[:, 3, 0:512]]
            for ct in range(2):
                y = yb_[ct]
                cw = lambda k, ct=ct: pvec[:, par, PV_CW + 3 * ct + k:PV_CW + 3 * ct + k + 1]
                mm_cols(slot_x, ct * 128, lambda k, s=s: hT[:, k, sl(s)], psb[0][:], [th(k, s) for k in range(8)], tps(0))
                mm_cols(slot_c, ct * 128, lambda k, s=s: hT[:, k, sl(s)], psb[1][:], [th(k, s) for k in range(8)], tps(1))
                a = tA_slot()
                P.op("act", [tps(0)], [("tA", a)], lambda e, a=a: e.activation(out=tAv(a), in_=psb[0][:], func=AF.Copy))
                P.op("dve", [("tA", a), tps(1)], [("tC", ct)], lambda e, a=a, y=y: e.tensor_tensor(
                    out=y[:, 2:514], in0=tAv(a), in1=psb[1][:], op=ALU.mult))
                a2 = tA_slot()
                P.op("act", [("tC", ct), ("pvec", par)], [("tA", a2)], lambda e, a2=a2, y=y, cw=cw: e.activation(
                    out=tAv(a2), in_=y[:, 2:514], func=AF.Identity, scale=cw(2)))
                P.op("dve", [("tC", ct), ("tA", a2), ("pvec", par)], [("tA", a2)], lambda e, a2=a2, y=y, cw=cw: e.scalar_tensor_tensor(
                    out=tAv(a2), in0=y[:, 1:513], scalar=cw(1), in1=tAv(a2), op0=ALU.mult, op1=ALU.add))
                P.op("dve", [("tC", ct), ("tA", a2), ("pvec", par)], [("tA", a2)], lambda e, a2=a2, y=y, cw=cw: e.scalar_tensor_tensor(
                    out=tAv(a2), in0=y[:, 0:512], scalar=cw(0), in1=tAv(a2), op0=ALU.mult, op1=ALU.add))
                P.op("dve", [("tC", ct)], [("tC", ct)], lambda e, y=y: e.tensor_copy(out=y[:, 0:2], in_=y[:, 512:514]))
                mm_cols(slot_b, ct * 128, lambda k, s=s: hT[:, k, sl(s)], psb[2 + ct][:], [th(k, s) for k in range(8)], tps(2 + ct))
                P.op("dve", [("tA", a2), tps(2 + ct)], [("tC", 2 + ct)], lambda e, a2=a2, ct=ct: e.tensor_tensor(
                    out=yc[ct], in0=tAv(a2), in1=psb[2 + ct][:], op=ALU.mult))
            if l == 0 and s == 0:
                tap("convyc", yc[0], [("tC", 2)])
            gnorm(l, 2, yc, [("tC", 2), ("tC", 3)], s, 512)
        RBr.prefetch()

    def mix_out(l, after_slice=None):
        par = l % 2
        slots4 = [RBr.next(("mo", l, jp), la=(3 if jp == 0 else 0)) for jp in range(4)]
        for s in range(4):
            for jp in range(4):
                slot = slots4[jp]
                for ji in range(2):
                    j = 2 * jp + ji
                    bank = out_bank()

                    def fo(e, slot=slot, s=s, ji=ji, bank=bank):
                        ins = None
                        for kk in range(8):
                            ins = e.matmul(psb[bank][:], RB[:, slot, kk, ji * 128:(ji + 1) * 128], G[:, kk, sl(s)],
                                           start=(kk == 0), stop=(kk == 7))
                        return ins
                    P.op("pe", [("RB", slot)] + [tg(kk, s) for kk in range(8)], [tps(bank)], fo)
                    P.op("dve", [tps(bank), tx(j, s), ("gate", par, 1)], [tx(j, s)],
                         lambda e, bank=bank, j=j, s=s: e.scalar_tensor_tensor(
                             out=xT[:, j, sl(s)], in0=psb[bank][:], scalar=gate[:, par, 1, j:j + 1],
                             in1=xT[:, j, sl(s)], op0=ALU.mult, op1=ALU.add))
            if after_slice is not None:
                after_slice(s)
        RBr.prefetch()

    out_evs = []
    hTf = hT[:].rearrange("p a b -> p (a b)").bitcast(F32)

    def final_p2(s, r):
        for jg in range(2):
            slots = []
            for jj in range(4):
                j = 4 * jg + jj
                a = tA_slot()
                slots.append(a)
                P.op("dve", [tx(j, s), ("rs", r), "fng"], [("tA", a)], lambda e, j=j, s=s, a=a, r=r: e.scalar_tensor_tensor(
                    out=tAv(a), in0=xT[:, j, sl(s)], scalar=fng[:, j:j + 1], in1=rs[:, r, :], op0=ALU.mult, op1=ALU.mult))
            for tl in range(4):
                tt = 4 * s + tl
                q = tt % 8
                stage = hTf[:, q * 1024:(q + 1) * 1024]
                bank = out_bank()

                def fe(e, slots=slots, tl=tl, bank=bank):
                    ins = None
                    for jj in range(4):
                        ins = e.transpose(psb[bank][:, jj * 128:(jj + 1) * 128], tAv(slots[jj])[:, tl * 128:(tl + 1) * 128], ident_f)
                    return ins
                P.op("pe", [("tA", a) for a in slots] + ["cst"], [tps(bank)], fe)
                wt = [("ostage", q, jg)] + [th(q, ss) for ss in range(4)]
                if (tt + jg) % 2 == 0:
                    P.op("act", [tps(bank)], wt, lambda e, stage=stage, jg=jg, bank=bank: e.activation(
                        out=stage[:, jg * 512:(jg + 1) * 512], in_=psb[bank][:], func=AF.Copy))
                else:
                    P.op("dve", [tps(bank)], wt, lambda e, stage=stage, jg=jg, bank=bank: e.tensor_copy(
                        out=stage[:, jg * 512:(jg + 1) * 512], in_=psb[bank][:]))
        for tl in range(4):
            tt = 4 * s + tl
            q = tt % 8
            stage = hTf[:, q * 1024:(q + 1) * 1024]
            ev = P.dma("sp", ("ost", q), [("ostage", q, 0), ("ostage", q, 1)] + [th(q, ss) for ss in range(4)], [],
                       [(out_d[tt * 128:(tt + 1) * 128, :], stage)])
            out_evs.append(ev)

    def final_cb():
        st = {}

        def cb(s):
            st[s] = stats_rstd(lambda k, s=s: xT[:, k, sl(s)], [tx(k, s) for k in range(8)], 8, 1.0 / D)
            if s >= 1:
                final_p2(s - 1, st[s - 1])
            if s == 3:
                final_p2(3, st[3])
        return cb

    for l in range(n_layers):
        P.new_epoch()
        if l + 1 < n_layers:
            load_layer_params(l + 1)
        if l == 0:
            norm_mod(l, 0)
            tap("h1", hT[:, 0, 0:512], [th(0, 0)])
        ffn(l, 0, after_slice=norm_cb(l, 1), pump=Pump(s5_setup_early(l)))
        if l == 0:
            tap("x1", xT[:, 0, 0:512], [tx(0, 0)])
        mixer_s5(l)
        mixer_sgu(l)
        mixer_pool(l)
        mixer_conv(l)
        if l == 0:
            tap("gall", G[:, :, 0:256], [tg(m, 0) for m in range(8)])
        mix_out(l, after_slice=norm_cb(l, 2))
        if l == 0:
            tap("x2", xT[:, 0, 0:512], [tx(0, 0)])
        if l + 1 < n_layers:
            ffn(l, 1, after_slice=norm_cb(l + 1, 0))
        else:
            ffn(l, 1, after_slice=final_cb())
        if l == 0:
            tap("x3", xT[:, 0, 0:512], [tx(0, 0)])

    for nm in taps_d:
        out_evs.append((None, ("D", ("tap", nm)), P.dcnt[("D", ("tap", nm))]))
    P.wait_all("sp", out_evs)

    sems = {}
    for i, sk in enumerate(P.semkeys):
        sems[sk] = es.enter_context(nc.semaphore("s%d" % i))
    with nc.Block() as block:
        P.emit(nc, block, sems)
    es.close()
    return nc


def _colmajor(v, ncols):
    return np.ascontiguousarray(np.swapaxes(v.reshape(v.shape[:-1] + (ncols, 128)), -1, -2))


def prepare_inputs(inp):
    f32 = np.float32
    g = {k: np.asarray(v, dtype=f32) for k, v in inp.items()}
    shared = {}
    for k in ["ada_w", "ffn1_w_in", "ffn1_w_out", "ffn2_w_in", "ffn2_w_out", "w_mix_in", "w_mix_out", "sgu_w"]:
        shared[k] = np.ascontiguousarray(g[k])
    shared["glu_w"] = np.ascontiguousarray(g["s5_glu_w"])
    shared["ada_bT"] = _colmajor(g["ada_b"], 72)
    shared["fng"] = _colmajor(g["final_norm_g"], 8)
    pv = np.zeros((L, 128, NPV), f32)
    pv[:, :, PV_N1:PV_N1 + 8] = _colmajor(g["norm1_g"], 8)
    pv[:, :, PV_N2:PV_N2 + 8] = _colmajor(g["norm2_g"], 8)
    pv[:, :, PV_N3:PV_N3 + 8] = _colmajor(g["norm3_g"], 8)
    pv[:, :, PV_MNG:PV_MNG + 8] = _colmajor(g["mix_norm_g"], 8)
    pv[:, :, PV_PSC:PV_PSC + 2] = _colmajor(g["pool_scale"], 2)
    cw = g["conv_w"]
    for ct in range(2):
        for k in range(3):
            pv[:, :, PV_CW + 3 * ct + k] = cw[:, k, ct * 128:(ct + 1) * 128]
    pv[:, :, PV_D:PV_D + 2] = _colmajor(g["s5_d"], 2)
    pv[:, :, PV_GB:PV_GB + 2] = _colmajor(g["s5_glu_b"], 2)
    pv[:, :, PV_LRE:PV_LRE + 8] = _colmajor(g["s5_lambda_re"].reshape(L, 1024), 8)
    pv[:, :, PV_LIM:PV_LIM + 8] = _colmajor(g["s5_lambda_im"].reshape(L, 1024), 8)
    ldt = np.repeat(g["s5_log_dt"], 64, axis=1)
    pv[:, :, PV_LDT:PV_LDT + 8] = _colmajor(ldt, 8)
    shared["pvec"] = pv
    sb_ = g["sgu_b"]
    sbt = np.zeros((L, 128, 2, 128), f32)
    for tile in range(2):
        for gl in range(2):
            sbt[:, gl * 64:(gl + 1) * 64, tile, :] = sb_[:, 2 * tile + gl, None, :]
    shared["sgu_bT"] = sbt
    pw = g["pool_w"]
    pbd = np.zeros((L, 128, 2, 128), f32)
    for tile in range(2):
        for gl in range(2):
            pbd[:, gl * 64:(gl + 1) * 64, tile, gl * 64:(gl + 1) * 64] = pw[:, 2 * tile + gl]
    shared["pool_bd"] = pbd
    bp = np.zeros((L, 2, 128, 8, 128), f32)
    cp = np.zeros((L, 2, 128, 8, 128), f32)
    for ri, (bn, cn) in enumerate([("s5_b_re", "s5_c_re"), ("s5_b_im", "s5_c_im")]):
        b = g[bn]
        c = g[cn]
        for i in range(8):
            for gl in range(2):
                gg = 2 * i + gl
                c0 = 32 * (i % 4) + 16 * gl
                bp[:, ri, gl * 64:(gl + 1) * 64, i, c0:c0 + 16] = b[:, gg]
                cp[:, ri, gl * 64:(gl + 1) * 64, i, c0:c0 + 16] = np.swapaxes(c[:, gg], -1, -2)
    shared["s5_Bp"] = bp
    shared["s5_Cp"] = cp
    cst = np.zeros((128, NCST), f32)
    cst[:, C_ID:C_ID + 128] = np.eye(128, dtype=f32)
    cst[:, C_TRI:C_TRI + 128] = np.triu(np.ones((128, 128), f32))
    cst[:, C_IOTA:C_IOTA + 256] = np.arange(256, dtype=f32)[None, :]
    wins = [2, 4, 8, 16]
    for tile in range(2):
        for gl in range(2):
            w = wins[2 * tile + gl]
            t = np.arange(16)
            cst[gl * 64:(gl + 1) * 64, C_ICNT + 16 * tile:C_ICNT + 16 * tile + 16] = 1.0 / np.minimum(t + 1, w)
            cst[gl * 64:(gl + 1) * 64, C_IW + tile] = 1.0 / w
    shared["cst"] = cst
    in_maps = []
    for b in range(8):
        m = dict(shared)
        m["x"] = np.ascontiguousarray(g["x"][b])
        m["cT"] = _colmajor(g["c"][b], 8)
        in_maps.append(m)
    return in_maps


_NC_CACHE = {}


def kernel(**inputs):
    in_maps = prepare_inputs(inputs)
    if "nc" not in _NC_CACHE:
        _NC_CACHE["nc"] = build_program()
    nc = _NC_CACHE["nc"]
    res = run_bass_kernel_spmd(nc, in_maps, core_ids=list(range(8)))
    out = np.stack([np.asarray(r["out"], dtype=np.float32) for r in res.results], axis=0)
    return out
```

```python
import math
from contextlib import ExitStack

import numpy as np
import concourse.bass as bass
import concourse.mybir as mybir
from concourse.bass_utils import run_bass_kernel_spmd

F32 = mybir.dt.float32
BF16 = mybir.dt.bfloat16
AF = mybir.ActivationFunctionType
ALU = mybir.AluOpType
AX = mybir.AxisListType

L = 4
D = 1024
S = 2048
DFF = 2816
NM = 22
PIN = 1792
EPS = 1e-6
TWO_PI = 2.0 * math.pi
MAGIC = 12582912.0
NPV = 68
PV_N1, PV_N2, PV_N3, PV_MNG, PV_PSC, PV_CW, PV_D, PV_GB, PV_LRE, PV_LIM, PV_LDT = 0, 8, 16, 24, 32, 34, 40, 42, 44, 52, 60
C_ID, C_TRI, C_IOTA, C_ICNT, C_IW, NCST = 0, 128, 256, 512, 544, 546

DEBUG = {}


class Prog:
    ENGS = ("pe", "act", "dve", "pool", "sp")

    def __init__(self):
        self.ops = {e: [] for e in self.ENGS}
        self.cnt = {e: 0 for e in self.ENGS}
        self.epoch = 0
        self.semkeys = []
        self.dcnt = {}
        self.last_w = {}
        self.readers = {}
        self.waited = {e: {} for e in self.ENGS}
        self.alias = {}

    def _exp(self, toks):
        out = []
        for t in toks:
            a = self.alias.get(t)
            if a is None:
                out.append(t)
            else:
                out.extend(a)
        return out

    def new_epoch(self):
        self.epoch += 1
        for e in self.ENGS:
            self.cnt[e] = 0

    def _sk(self, k):
        if k not in self.dcnt:
            self.dcnt[k] = 0
            self.semkeys.append(k)
        return k

    def _waits(self, eng, reads, writes):
        need = {}

        def add(ev, is_read):
            peng, sk, val = ev
            if peng == eng and eng == "pe" and not is_read:
                return
            if need.get(sk, 0) < val:
                need[sk] = val

        for t in reads:
            w = self.last_w.get(t)
            if w is not None:
                add(w, True)
        for t in writes:
            w = self.last_w.get(t)
            if w is not None:
                add(w, False)
            for ev in self.readers.get(t, {}).values():
                add(ev, False)
        out = []
        wd = self.waited[eng]
        for sk, val in need.items():
            if wd.get(sk, 0) < val:
                wd[sk] = val
                out.append((sk, val))
        return out

    def _record(self, ev, reads, writes):
        for t in reads:
            d = self.readers.setdefault(t, {})
            old = d.get(ev[1])
            if old is None or old[2] < ev[2]:
                d[ev[1]] = ev
        for t in writes:
            self.last_w[t] = ev
            self.readers[t] = {}

    def op(self, eng, reads, writes, fn):
        reads = self._exp(reads)
        writes = self._exp(writes)
        waits = self._waits(eng, reads, writes)
        sk = self._sk(("E", eng, self.epoch))
        self.cnt[eng] += 1
        self.dcnt[sk] = self.cnt[eng]
        ev = (eng, sk, self.cnt[eng])
        self.ops[eng].append(("op", waits, fn, sk))
        self._record(ev, reads, writes)
        return ev

    def dma(self, eng, semkey, reads, writes, pairs):
        reads = self._exp(reads)
        writes = self._exp(writes)
        waits = self._waits(eng, reads, writes)
        sk = self._sk(("D", semkey))
        self.dcnt[sk] += 16 * len(pairs)
        ev = (None, sk, self.dcnt[sk])
        self.ops[eng].append(("dma", waits, pairs, sk))
        self._record(ev, reads, writes)
        return ev

    def wait_all(self, eng, evs):
        waits = []
        for (_, sk, val) in evs:
            waits.append((sk, val))
        self.ops[eng].append(("wait", waits, None, None))

    def emit(self, nc, block, sems):
        engmap = {"pe": block.tensor, "act": block.scalar, "dve": block.vector,
                  "pool": block.gpsimd, "sp": block.sync}
        for eng in self.ENGS:
            ops = self.ops[eng]
            if not ops:
                continue

            def body(e, ops=ops):
                for kind, waits, fn, sk in ops:
                    for (wsk, val) in waits:
                        e.wait_ge(sems[wsk], val)
                    if kind == "op":
                        ins = fn(e)
                        ins.then_inc(sems[sk], 1)
                    elif kind == "dma":
                        for (o, i) in fn:
                            e.dma_start(out=o, in_=i).then_inc(sems[sk], 16)

            engmap[eng](body)


class Ring:
    def __init__(self, P, name, nslots, chunks):
        self.P = P
        self.name = name
        self.n = nslots
        self.chunks = chunks
        self.loaded = 0
        self.cur = 0

    def _emit_load(self, i):
        slot = i % self.n
        ch = self.chunks[i]
        self.P.dma("pool", (self.name, slot), [], [(self.name, slot)], ch["pairs"](slot))

    def prefetch(self):
        hi = min(self.cur + self.n - 1, len(self.chunks) - 1)
        j = self.loaded
        while j <= hi:
            if self.chunks[j].get("barrier"):
                break
            self._emit_load(j)
            j += 1
        self.loaded = max(self.loaded, j)

    def next(self, tag, la=None):
        i = self.cur
        assert self.chunks[i]["tag"] == tag, (self.chunks[i]["tag"], tag)
        if la is None:
            la = self.n - 1
        hi = min(i + la, len(self.chunks) - 1)
        j = self.loaded
        while j <= hi:
            if j > i and self.chunks[j].get("barrier"):
                break
            self._emit_load(j)
            j += 1
        self.loaded = max(self.loaded, j)
        self.cur += 1
        return i % self.n


def build_program(n_layers=L, debug_taps=()):
    nc = bass.Bass("TRN2", target_bir_lowering=False)
    P = Prog()
    for b in range(8):
        P.alias[("psh", b, 0)] = [("ps", b)]
        P.alias[("psh", b, 1)] = [("ps", b)]
    for a in range(4):
        P.alias[("tA", a)] = [("tA8", 2 * a), ("tA8", 2 * a + 1)]
        P.alias[("tB", a)] = [("tB8", 2 * a), ("tB8", 2 * a + 1)]

    def din(name, shape):
        return nc.dram_tensor(name, list(shape), F32, kind="ExternalInput").ap()

    x_d = din("x", [S, D])
    cT_d = din("cT", [128, 8])
    cst_d = din("cst", [128, NCST])
    fng_d = din("fng", [128, 8])
    ada_w_d = din("ada_w", [L, D, 9 * D])
    ada_bT_d = din("ada_bT", [L, 128, 72])
    pvec_d = din("pvec", [L, 128, NPV])
    w1i_d = din("ffn1_w_in", [L, D, 2 * DFF])
    w1o_d = din("ffn1_w_out", [L, DFF, D])
    w2i_d = din("ffn2_w_in", [L, D, 2 * DFF])
    w2o_d = din("ffn2_w_out", [L, DFF, D])
    wmi_d = din("w_mix_in", [L, D, PIN])
    wmo_d = din("w_mix_out", [L, D, D])
    sguw_d = din("sgu_w", [L, 4, 128, 128])
    sgub_d = din("sgu_bT", [L, 128, 2, 128])
    pool_d = din("pool_bd", [L, 128, 2, 128])
    bp_d = din("s5_Bp", [L, 2, 128, 8, 128])
    cp_d = din("s5_Cp", [L, 2, 128, 8, 128])
    gluw_d = din("glu_w", [L, 256, 256])
    out_d = nc.dram_tensor("out", [S, D], F32, kind="ExternalOutput").ap()
    taps_d = {}
    for (nm, shp) in debug_taps:
        taps_d[nm] = nc.dram_tensor("tap_" + nm, list(shp), F32, kind="ExternalOutput").ap()

    es = ExitStack()

    def sb(name, shape, dt):
        return es.enter_context(nc.sbuf_tensor("sb_" + name, list(shape), dt))

    xT = sb("xT", [128, 8, S], F32)
    hT = sb("hT", [128, 8, S], BF16)
    G = sb("G", [128, 8, S], BF16)
    RA = sb("RA", [128, 3, 8, 512], BF16)
    RB = sb("RB", [128, 4, 8, 256], BF16)
    tA = sb("tA", [128, 2048], F32)
    tB = sb("tB", [128, 4, 512], BF16)
    rs = sb("rs", [128, 2, 512], F32)
    tC = sb("tC", [128, 4, 528], F32)
    cst = sb("cst", [128, NCST], F32)
    ident_b = sb("ident_b", [128, 128], BF16)
    ones_b = sb("ones_b", [128, 128], BF16)
    epsc = sb("epsc", [128, 1], F32)
    cT_f = sb("cT_f", [128, 8], F32)
    cact_b = sb("cact_b", [128, 8], BF16)
    condT = sb("condT", [128, 2, 72], F32)
    adab = sb("adab", [128, 2, 72], F32)
    modA = sb("modA", [128, 2, 3, 8], F32)
    gate = sb("gate", [128, 2, 3, 8], F32)
    pvec = sb("pvec", [128, 2, NPV], F32)
    fng = sb("fng", [128, 8], F32)
    WsT = sb("WsT", [128, 4, 128], BF16)
    bsT = sb("bsT", [128, 2, 128], F32)
    poolw = sb("poolw", [128, 2, 128], BF16)
    gluw = sb("gluw", [128, 2, 256], BF16)
    diagD = sb("diagD", [128, 2, 128], BF16)
    s5s = sb("s5s", [128, 16, 8], F32)
    s5i = sb("s5i", [128, 2, 8], F32)
    smal = sb("smal", [128, 40], F32)
    nCTim = sb("nCTim", [128, 8, 128], BF16)
    crowd = sb("crowd", [1, 512], F32)
    g5sq = sb("g5sq", [128, 2, 256], BF16)
    g5rs = sb("g5rs", [128, 256], F32)

    psb = [es.enter_context(nc.psum_tensor("ps%d" % i, [128, 512], F32)) for i in range(8)]

    ident_f = cst[:, C_ID:C_ID + 128]
    trimask = cst[:, C_TRI:C_TRI + 128]
    iota = cst[:, C_IOTA:C_IOTA + 256]
    one_f = cst[0:1, 0:1]

    Gf = G[:].rearrange("p a b -> p (a b)").bitcast(F32)
    RAf = RA[:].rearrange("p a b c -> p (a b c)").bitcast(F32)
    tab = RAf[:, 0:4096].rearrange("p (i c t) -> p i c t", i=8, c=2)
    RA2 = RA[:, 2].rearrange("p a b -> p (a b)")
    BbT = [RA2[:, 0:1024].rearrange("p (i c) -> p i c", i=8), RA2[:, 1024:2048].rearrange("p (i c) -> p i c", i=8)]
    CTb = [RA2[:, 2048:3072].rearrange("p (i c) -> p i c", i=8), RA2[:, 3072:4096].rearrange("p (i c) -> p i c", i=8)]

    def sl(s, w=512):
        return slice(s * w, (s + 1) * w)

    ring_state = {"AB": 0, "OUT": 0}

    def ab_pair():
        i = ring_state["AB"]
        ring_state["AB"] = (i + 1) % 2
        return 2 * i, 2 * i + 1

    def out_bank():
        i = ring_state["OUT"]
        ring_state["OUT"] = (i + 1) % 2
        return 4 + i

    FFN_PARTS = [(0, 6), (6, 8), (14, 8)]

    def ra_pairs_win(wd, l, m):
        def f(slot):
            src = wd[l].rearrange("(k p) n -> p k n", p=128)
            return [(RA[:, slot, :, 0:256], src[:, :, m * 128:m * 128 + 256]),
                    (RA[:, slot, :, 256:512], src[:, :, DFF + m * 128:DFF + m * 128 + 256])]
        return f

    def ra_pairs_ada(l, cc):
        def f(slot):
            src = ada_w_d[l].rearrange("(k p) n -> p k n", p=128)
            return [(RA[:, slot, :, :], src[:, :, cc * 512:(cc + 1) * 512])]
        return f

    def rb_pairs_wout(wd, l, m0, mc, jp):
        def f(slot):
            src = wd[l].rearrange("(m p) n -> p m n", p=128)
            return [(RB[:, slot, 0:mc, :], src[:, m0:m0 + mc, jp * 256:(jp + 1) * 256])]
        return f

    def rb_pairs_cols(wd, l, c0):
        def f(slot):
            src = wd[l].rearrange("(k p) n -> p k n", p=128)
            return [(RB[:, slot, :, :], src[:, :, c0:c0 + 256])]
        return f

    MIX_CHUNKS = [("D", 1536), ("Au", 0), ("Av", 256), ("B", 512), ("Cc", 1024), ("Cx", 1280), ("Cb", 768)]

    def ada_after(l, f, ci):
        out = []
        if l == 0 and f == 0:
            out.append((0, 6 + ci))
            if ci == 10:
                out.append((0, 17))
        if l + 1 < n_layers and ci < 9:
            out.append((l + 1, f * 9 + ci))
        return out

    ra_chunks = []
    rb_chunks = []
    for cc in range(6):
        ra_chunks.append(dict(tag=("ada", 0, cc), pairs=ra_pairs_ada(0, cc)))
    for l in range(n_layers):
        for f in range(2):
            wi = w1i_d if f == 0 else w2i_d
            wo = w1o_d if f == 0 else w2o_d
            ci = 0
            first = True
            for (m0, mc) in FFN_PARTS:
                for c in range(mc // 2):
                    ra_chunks.append(dict(tag=("win", l, f, m0 + 2 * c), pairs=ra_pairs_win(wi, l, m0 + 2 * c),
                                          barrier=(first and f == 1)))
                    first = False
                    for (al, acc) in ada_after(l, f, ci):
                        ra_chunks.append(dict(tag=("ada", al, acc), pairs=ra_pairs_ada(al, acc)))
                    ci += 1
                for jp in range(4):
                    rb_chunks.append(dict(tag=("wout", l, f, m0, jp), pairs=rb_pairs_wout(wo, l, m0, mc, jp)))
            if f == 0:
                for (nm, c0) in MIX_CHUNKS:
                    rb_chunks.append(dict(tag=("mi", l, nm), pairs=rb_pairs_cols(wmi_d, l, c0)))
                for jp in range(4):
                    rb_chunks.append(dict(tag=("mo", l, jp), pairs=rb_pairs_cols(wmo_d, l, jp * 256)))
    RAr = Ring(P, "RA", 3, ra_chunks)
    RBr = Ring(P, "RB", 4, rb_chunks)

    def tx(j, s): return ("x", j, s)
    def th(k, s): return ("h", k, s)
    def tg(m, s): return ("G", m, s)
    def tps(b): return ("ps", b)

    P.dma("sp", "c0a", [], ["cst"], [(cst[:], cst_d)])
    P.dma("sp", "c0b", [], ["cT_f"], [(cT_f[:], cT_d)])
    P.dma("sp", "c0c", [], ["fng"], [(fng[:], fng_d)])
    P.op("dve", [], ["ones_b"], lambda e: e.memset(ones_b[:], 1.0))
    P.op("dve", [], ["epsc"], lambda e: e.memset(epsc[:], EPS))
    P.op("dve", ["cst"], ["ident_b"], lambda e: e.tensor_copy(out=ident_b[:], in_=ident_f))
    P.op("act", ["cT_f"], ["cact_b"], lambda e: e.activation(out=cact_b[:], in_=cT_f[:], func=AF.Silu))

    def load_layer_params(l):
        par = l % 2
        P.dma("sp", ("pv", par), [], [("pvec", par)], [(pvec[:, par, :], pvec_d[l])])
        P.dma("sp", ("ab", par), [], [("adab", par)], [(adab[:, par, :], ada_bT_d[l])])

    crow = crowd[0:1, 0:512]
    cond_pending = []

    def cond_flush():
        while cond_pending:
            cond_pending.pop(0)()

    def cond_chunk(l, cc):
        cond_flush()
        slot = RAr.next(("ada", l, cc))

        def f1(e):
            ins = None
            for k in range(8):
                ins = e.matmul(psb[6][0:1, :], cact_b[:, k:k + 1], RA[:, slot, k, :], start=(k == 0), stop=(k == 7))
            return ins
        P.op("pe", [("RA", slot), "cact_b"], [tps(6)], f1)
        P.op("act", [tps(6)], ["crow"], lambda e: e.activation(out=crow, in_=psb[6][0:1, :], func=AF.Copy))

        def f2(e):
            ins = None
            for q in range(4):
                col = cc * 4 + q
                ins = e.matmul(psb[7][:, col:col + 1], crow[0:1, q * 128:(q + 1) * 128], one_f, start=True, stop=True)
            return ins
        cond_pending.append(lambda: P.op("pe", ["crow", "cst"], [("psC",)], f2))

    def cond_finish_n(l, n):
        par = l % 2
        c0, c1 = 24 * n, 24 * n + 24
        P.op("dve", [("psC",), ("adab", par)], [("cond", par, n)],
             lambda e: e.tensor_tensor(out=condT[:, par, c0:c1], in0=psb[7][:, c0:c1], in1=adab[:, par, c0:c1], op=ALU.add))
        scv = condT[:, par, (3 * n + 1) * 8:(3 * n + 2) * 8]
        gv = pvec[:, par, PV_N1 + 8 * n:PV_N1 + 8 * n + 8]
        gt = condT[:, par, (3 * n + 2) * 8:(3 * n + 3) * 8]
        P.op("dve", [("cond", par, n), ("pvec", par)], [("modA", par, n)],
             lambda e: e.scalar_tensor_tensor(
                 out=modA[:, par, n, :], in0=scv, scalar=1.0, in1=gv, op0=ALU.add, op1=ALU.mult))
        fac = 1.0 if n == 1 else 0.5
        P.op("dve", [("cond", par, n)], [("gate", par, n)],
             lambda e: e.tensor_scalar(out=gate[:, par, n, :], in0=gt, scalar1=fac, scalar2=None, op0=ALU.mult))

    def cond_finish(l):
        for n in range(3):
            cond_finish_n(l, n)

    def ada_emit(al, acc):
        cond_chunk(al, acc)
        if acc in (11, 17):
            cond_flush()
        if al == 0 and acc == 11:
            cond_finish_n(0, 1)
        elif al == 0 and acc == 17:
            cond_finish_n(0, 2)
        elif al > 0 and acc == 17:
            cond_finish(al)

    load_layer_params(0)
    for tt in range(16):
        q = tt % 8
        stage = Gf[:, q * 1024:(q + 1) * 1024]
        P.dma("sp", ("xs", q), [], [tg(q, s) for s in range(4)], [(stage, x_d[tt * 128:(tt + 1) * 128, :])])
        for jg in range(2):
            bank = 2 * (tt % 2) + jg

            def ft(e, stage=stage, jg=jg, bank=bank):
                ins = None
                for jj in range(4):
                    j = 4 * jg + jj
                    ins = e.transpose(psb[bank][:, jj * 128:(jj + 1) * 128], stage[:, j * 128:(j + 1) * 128], ident_f)
                return ins
            P.op("pe", [tg(q, s) for s in range(4)] + ["cst"], [tps(bank)], ft)
            dst = xT[:, 4 * jg:4 * jg + 4, tt * 128:(tt + 1) * 128]
            src = psb[bank][:].rearrange("p (a b) -> p a b", a=4)
            eng = "act" if jg == 0 else "dve"
            if eng == "act":
                P.op("act", [tps(bank)], [tx(4 * jg + jj, tt // 4) for jj in range(4)],
                     lambda e, dst=dst, src=src: e.activation(out=dst, in_=src, func=AF.Copy))
            else:
                P.op("dve", [tps(bank)], [tx(4 * jg + jj, tt // 4) for jj in range(4)],
                     lambda e, dst=dst, src=src: e.tensor_copy(out=dst, in_=src))

    for cc in range(6):
        cond_chunk(0, cc)
    cond_flush()
    cond_finish_n(0, 0)
    RBr.prefetch()

    cnt = {"tA": 0, "tB": 0, "rs": 0}

    def tA_slot():
        i = cnt["tA"]
        cnt["tA"] = (i + 1) % 4
        return i

    def tB_slot():
        i = cnt["tB"]
        cnt["tB"] = (i + 1) % 4
        return i

    def rs_slot():
        i = cnt["rs"]
        cnt["rs"] = (i + 1) % 2
        return i

    def tAv(i, w=512):
        return tA[:, i * 512:i * 512 + w]

    def stats_rstd(src_fn, src_tokens, ntiles, inv_n, w=512):
        r = rs_slot()
        for k in range(ntiles):
            b = tB_slot()
            P.op("act", [src_tokens[k]], [("tB", b)],
                 lambda e, k=k, b=b: e.activation(out=tB[:, b, 0:w], in_=src_fn(k), func=AF.Square))
            P.op("pe", [("tB", b), "ones_b"], [tps(6)],
                 lambda e, k=k, b=b: e.matmul(psb[6][:, 0:w], ones_b[:], tB[:, b, 0:w], start=(k == 0), stop=(k == ntiles - 1)))
        P.op("act", [tps(6), "epsc"], [("rs", r)],
             lambda e: e.activation(out=rs[:, r, 0:w], in_=psb[6][:, 0:w], func=AF.Ln, bias=epsc[:, 0:1], scale=inv_n))
        P.op("act", [("rs", r)], [("rs", r)],
             lambda e: e.activation(out=rs[:, r, 0:w], in_=rs[:, r, 0:w], func=AF.Exp, scale=-0.5))
        return r

    def norm_p1(l, n, s):
        return stats_rstd(lambda k, s=s: xT[:, k, sl(s)], [tx(k, s) for k in range(8)], 8, 1.0 / D)

    def norm_p2(l, n, s, r):
        par = l % 2
        for k in range(8):
            a = tA_slot()
            P.op("dve", [tx(k, s), ("rs", r)], [("tA", a)],
                 lambda e, k=k, s=s, a=a, r=r: e.tensor_tensor(out=tAv(a), in0=xT[:, k, sl(s)], in1=rs[:, r, :], op=ALU.mult))
            P.op("act", [("tA", a), ("modA", par, n), ("cond", par, n)], [th(k, s)],
                 lambda e, k=k, s=s, a=a: e.activation(
                     out=hT[:, k, sl(s)], in_=tAv(a), func=AF.Identity,
                     bias=condT[:, par, 3 * n * 8 + k:3 * n * 8 + k + 1], scale=modA[:, par, n, k:k + 1]))

    def norm_slice(l, n, s):
        r = norm_p1(l, n, s)
        norm_p2(l, n, s, r)

    def norm_cb(l, n):
        st = {}

        def cb(s):
            st[s] = norm_p1(l, n, s)
            if s >= 1:
                norm_p2(l, n, s - 1, st[s - 1])
            if s == 3:
                norm_p2(l, n, 3, st[3])
        return cb

    def norm_mod(l, n):
        for s in range(4):
            norm_slice(l, n, s)

    def ffn(l, f, after_slice=None, pump=None):
        par = l % 2
        n = 0 if f == 0 else 2
        ada_i = 0
        ra_last = None
        if pump is not None:
            j = RAr.cur
            while j + 1 < len(RAr.chunks) and not RAr.chunks[j + 1].get("barrier"):
                j += 1
            ra_last = j

        def ra_done():
            ci = RAr.cur - 1
            if pump is not None and ci + 3 > ra_last:
                pump.free[ci % 3] = True
        for (m0, mc) in FFN_PARTS:
            lastpart = (m0 + mc == NM)
            for c in range(mc // 2):
                slot = RAr.next(("win", l, f, m0 + 2 * c))
                for s in range(4):
                    for mi in range(2):
                        mloc = 2 * c + mi
                        ba, bb = ab_pair()

                        def fa(e, slot=slot, s=s, col=mi * 128, bank=ba):
                            ins = None
                            for k in range(8):
                                ins = e.matmul(psb[bank][:], RA[:, slot, k, col:col + 128], hT[:, k, sl(s)],
                                               start=(k == 0), stop=(k == 7))
                            return ins
                        P.op("pe", [("RA", slot)] + [th(k, s) for k in range(8)], [tps(ba)], fa)

                        def fb(e, slot=slot, s=s, col=256 + mi * 128, bank=bb):
                            ins = None
                            for k in range(8):
                                ins = e.matmul(psb[bank][:], RA[:, slot, k, col:col + 128], hT[:, k, sl(s)],
                                               start=(k == 0), stop=(k == 7))
                            return ins
                        P.op("pe", [("RA", slot)] + [th(k, s) for k in range(8)], [tps(bb)], fb)
                        a = tA_slot()
                        P.op("act", [tps(ba)], [("tA", a)],
                             lambda e, a=a, ba=ba: e.activation(out=tAv(a), in_=psb[ba][:], func=AF.Silu))
                        P.op("dve", [("tA", a), tps(bb)], [tg(mloc, s)],
                             lambda e, a=a, bb=bb, mloc=mloc, s=s: e.tensor_tensor(
                                 out=G[:, mloc, sl(s)], in0=tAv(a), in1=psb[bb][:], op=ALU.mult))
                        if lastpart and pump is not None:
                            pump.step(4)
                        if s == 0 and mi == 1:
                            cond_flush()
                ra_done()
                for (al, acc) in ada_after(l, f, ada_i):
                    ada_emit(al, acc)
                    ra_done()
                ada_i += 1
            cond_flush()
            if lastpart:
                slots4 = [RBr.next(("wout", l, f, m0, jp), la=(3 if jp == 0 else 0)) for jp in range(4)]
                order = [(jp, s) for s in range(4) for jp in range(4)]
            else:
                slots4 = None
                order = [(jp, s) for jp in range(4) for s in range(4)]
            slot = None
            for (jp, s) in order:
                if lastpart:
                    slot = slots4[jp]
                elif s == 0:
                    slot = RBr.next(("wout", l, f, m0, jp))
                if True:
                    for ji in range(2):
                        j = 2 * jp + ji
                        bank = out_bank()

                        def fo(e, slot=slot, s=s, ji=ji, bank=bank, mc=mc):
                            ins = None
                            for mm in range(mc):
                                ins = e.matmul(psb[bank][:], RB[:, slot, mm, ji * 128:(ji + 1) * 128], G[:, mm, sl(s)],
                                               start=(mm == 0), stop=(mm == mc - 1))
                            return ins
                        P.op("pe", [("RB", slot)] + [tg(mm, s) for mm in range(mc)], [tps(bank)], fo)
                        P.op("dve", [tps(bank), tx(j, s), ("gate", par, n)], [tx(j, s)],
                             lambda e, bank=bank, j=j, s=s: e.scalar_tensor_tensor(
                                 out=xT[:, j, sl(s)], in0=psb[bank][:], scalar=gate[:, par, n, j:j + 1],
                                 in1=xT[:, j, sl(s)], op0=ALU.mult, op1=ALU.add))
                if lastpart and pump is not None:
                    pump.step(4)
                if lastpart and jp == 3 and after_slice is not None:
                    after_slice(s)
            if lastpart and pump is not None:
                pump.flush()
            if lastpart:
                RBr.prefetch()

    def gnorm(l, q, srcs, src_tokens, s, w):
        par = l % 2
        r = stats_rstd(lambda k: srcs[k], src_tokens, 2, 1.0 / 256, w=w)
        for ct in range(2):
            P.op("dve", [src_tokens[ct], ("rs", r), ("pvec", par)], [tg(2 * q + ct, (s * w) // 512)],
                 lambda e, ct=ct: e.scalar_tensor_tensor(
                     out=G[:, 2 * q + ct, s * w:(s + 1) * w], in0=srcs[ct],
                     scalar=pvec[:, par, PV_MNG + 2 * q + ct:PV_MNG + 2 * q + ct + 1],
                     in1=rs[:, r, 0:w], op0=ALU.mult, op1=ALU.mult))

    def mm_cols(slot, c0, rhs_fn, bank_ap, reads, wtok):
        def fm(e):
            ins = None
            for k in range(8):
                ins = e.matmul(bank_ap, RB[:, slot, k, c0:c0 + 128], rhs_fn(k), start=(k == 0), stop=(k == 7))
            return ins
        P.op("pe", [("RB", slot)] + reads, [wtok], fm)

    def s5_setup_early(l):
        lists = {k: [] for k in ("free", "fold", "tab0", "tab1", "mat", "c256")}
        cur = {"name": "free", "gate": None}
        Q = lambda *a: lists[cur["name"]].append((cur["gate"], lambda a=a: P.op(*a)))
        QD = lambda *a: lists[cur["name"]].append((cur["gate"], lambda a=a: P.dma(*a)))
        par = l % 2
        pv = pvec[:, par, :]
        lre = pv[:, PV_LRE:PV_LRE + 8]
        lim = pv[:, PV_LIM:PV_LIM + 8]
        ldt = pv[:, PV_LDT:PV_LDT + 8]
        T = lambda i: s5s[:, i, :]
        tok = ("s5s",)
        rd = [("pvec", par), tok]
        dv = lambda fn: Q("dve", rd, [tok], fn)
        ac = lambda fn: Q("act", rd, [tok], fn)
        cur.update(name="mat", gate=2)
        QD("pool", ("s5c", 0), [], [("RA", 2)], [(CTb[0], cp_d[l, 0]), (CTb[1], cp_d[l, 1])])
        Q("act", [("RA", 2)], ["nCTim"], lambda e: e.activation(out=nCTim[:], in_=CTb[1], func=AF.Identity, scale=-1.0))
        cur.update(name="free", gate=None)
        QD("pool", ("s5c", 1), [], ["gluw"], [(gluw[:], gluw_d[l].rearrange("(k p) n -> p k n", p=128))])
        QD("pool", ("s5c", 2), [], ["poolw"], [(poolw[:], pool_d[l])])
        QD("sp", ("s5b", 1), [], ["bsT"], [(bsT[:], sgub_d[l])])
        ac(lambda e: e.activation(out=T(0), in_=ldt, func=AF.Exp))
        dv(lambda e: e.tensor_tensor(out=T(1), in0=lim, in1=T(0), op=ALU.mult))
        dv(lambda e: e.tensor_tensor(out=T(10), in0=lre, in1=T(0), op=ALU.mult))
        ac(lambda e: e.activation(out=T(2), in_=T(10), func=AF.Exp))
        dv(lambda e: e.tensor_scalar(out=T(10), in0=T(1), scalar1=1.0 / TWO_PI, scalar2=MAGIC, op0=ALU.mult, op1=ALU.add))
        dv(lambda e: e.tensor_scalar(out=T(10), in0=T(10), scalar1=MAGIC, scalar2=None, op0=ALU.subtract))
        dv(lambda e: e.scalar_tensor_tensor(out=T(1), in0=T(10), scalar=-TWO_PI, in1=T(1), op0=ALU.mult, op1=ALU.add))
        dv(lambda e: e.tensor_scalar(out=T(1), in0=T(1), scalar1=math.pi, scalar2=-math.pi, op0=ALU.min, op1=ALU.max))
        ac(lambda e: e.activation(out=T(3), in_=T(1), func=AF.Sin))
        dv(lambda e: e.tensor_scalar(out=T(10), in0=T(1), scalar1=1.0 / TWO_PI, scalar2=0.25, op0=ALU.mult, op1=ALU.add))
        dv(lambda e: e.tensor_scalar(out=T(11), in0=T(10), scalar1=MAGIC, scalar2=MAGIC, op0=ALU.add, op1=ALU.subtract))
        dv(lambda e: e.tensor_tensor(out=T(10), in0=T(10), in1=T(11), op=ALU.subtract))
        dv(lambda e: e.tensor_scalar(out=T(10), in0=T(10), scalar1=TWO_PI, scalar2=None, op0=ALU.mult))
        dv(lambda e: e.tensor_scalar(out=T(10), in0=T(10), scalar1=math.pi, scalar2=-math.pi, op0=ALU.min, op1=ALU.max))
        ac(lambda e: e.activation(out=T(4), in_=T(10), func=AF.Sin))
        dv(lambda e: e.tensor_tensor(out=T(5), in0=T(2), in1=T(4), op=ALU.mult))
        dv(lambda e: e.tensor_tensor(out=T(6), in0=T(2), in1=T(3), op=ALU.mult))
        dv(lambda e: e.tensor_tensor(out=T(7), in0=lre, in1=lre, op=ALU.mult))
        dv(lambda e: e.tensor_tensor(out=T(10), in0=lim, in1=lim, op=ALU.mult))
        dv(lambda e: e.tensor_tensor(out=T(7), in0=T(7), in1=T(10), op=ALU.add))
        dv(lambda e: e.reciprocal(out=T(7), in_=T(7)))
        dv(lambda e: e.tensor_scalar(out=T(11), in0=T(5), scalar1=-1.0, scalar2=None, op0=ALU.add))
        dv(lambda e: e.tensor_tensor(out=T(8), in0=T(11), in1=lre, op=ALU.mult))
        dv(lambda e: e.tensor_tensor(out=T(10), in0=T(6), in1=lim, op=ALU.mult))
        dv(lambda e: e.tensor_tensor(out=T(8), in0=T(8), in1=T(10), op=ALU.add))
        dv(lambda e: e.tensor_tensor(out=T(8), in0=T(8), in1=T(7), op=ALU.mult))
        dv(lambda e: e.tensor_tensor(out=T(9), in0=T(6), in1=lre, op=ALU.mult))
        dv(lambda e: e.tensor_tensor(out=T(10), in0=T(11), in1=lim, op=ALU.mult))
        dv(lambda e: e.tensor_tensor(out=T(9), in0=T(9), in1=T(10), op=ALU.subtract))
        dv(lambda e: e.tensor_tensor(out=T(9), in0=T(9), in1=T(7), op=ALU.mult))
        SIN_SC = TWO_PI * (1.0 - 1e-6)
        scn = g5rs[:]
        for i in range(8):
            hf = i // 4
            cur.update(name="tab%d" % hf, gate=hf)
            th_i = s5s[:, 1, i:i + 1]
            rtok = [("RA", hf)]
            ts_, tc_ = tab[:, i, 1, :], tab[:, i, 0, :]
            Q("dve", [tok, "cst"], rtok,
              lambda e, th_i=th_i, ts_=ts_: e.tensor_scalar(out=ts_, in0=iota, scalar1=th_i, scalar2=1.0 / TWO_PI, op0=ALU.mult, op1=ALU.mult))
            Q("dve", rtok, [("g5rs",)],
              lambda e, ts_=ts_: e.tensor_scalar(out=scn, in0=ts_, scalar1=MAGIC, scalar2=MAGIC, op0=ALU.add, op1=ALU.subtract))
            Q("dve", rtok + [("g5rs",)], rtok,
              lambda e, ts_=ts_: e.tensor_tensor(out=ts_, in0=ts_, in1=scn, op=ALU.subtract))
            Q("dve", rtok, rtok,
              lambda e, ts_=ts_, tc_=tc_: e.tensor_scalar(out=tc_, in0=ts_, scalar1=0.25, scalar2=None, op0=ALU.add))
            Q("dve", rtok, [("g5rs",)],
              lambda e, tc_=tc_: e.tensor_scalar(out=scn, in0=tc_, scalar1=MAGIC, scalar2=MAGIC, op0=ALU.add, op1=ALU.subtract))
            Q("dve", rtok + [("g5rs",)], rtok,
              lambda e, tc_=tc_: e.tensor_tensor(out=tc_, in0=tc_, in1=scn, op=ALU.subtract))
            if i % 4 == 3:
                tv = RAf[:, hf * 2048:(hf + 1) * 2048]
                Q("act", [("RA", hf)], [("RA", hf)], lambda e, tv=tv: e.activation(out=tv, in_=tv, func=AF.Sin, scale=SIN_SC))
        cur.update(name="c256", gate=None)
        c255 = tab[:, :, 0, 255]
        s255 = tab[:, :, 1, 255]
        rd2 = [tok, ("RA", 0), ("RA", 1)]
        Q("dve", rd2, [tok], lambda e: e.tensor_tensor(out=T(12), in0=c255, in1=T(4), op=ALU.mult))
        Q("dve", rd2, [tok], lambda e: e.tensor_tensor(out=T(10), in0=s255, in1=T(3), op=ALU.mult))
        Q("dve", rd2, [tok], lambda e: e.tensor_tensor(out=T(12), in0=T(12), in1=T(10), op=ALU.subtract))
        Q("dve", rd2, [tok], lambda e: e.tensor_tensor(out=T(13), in0=s255, in1=T(4), op=ALU.mult))
        Q("dve", rd2, [tok], lambda e: e.tensor_tensor(out=T(10), in0=c255, in1=T(3), op=ALU.mult))
        Q("dve", rd2, [tok], lambda e: e.tensor_tensor(out=T(13), in0=T(13), in1=T(10), op=ALU.add))
        Q("dve", rd2, [tok], lambda e: e.tensor_scalar(out=T(14), in0=T(13), scalar1=-1.0, scalar2=None, op0=ALU.mult))
        cur.update(name="free", gate=None)
        Q("dve", [], [("s5i", i) for i in range(8)], lambda e: e.memset(s5i[:], 0.0))
        for ct in range(2):
            Q("act", ["cst", ("pvec", par)], ["diagD"], lambda e, ct=ct: e.activation(
                out=diagD[:, ct, :], in_=ident_f, func=AF.Identity, scale=pvec[:, par, PV_D + ct:PV_D + ct + 1]))
        cur.update(name="fold", gate=None)
        tCf = tC[:].rearrange("p a b -> p (a b)")
        t4 = [("tC", j) for j in range(4)]
        for hf in range(2):
            Bp = [tCf[:, 0:512].rearrange("p (i c) -> p i c", i=4), tCf[:, 512:1024].rearrange("p (i c) -> p i c", i=4)]
            Bb = [tCf[:, 1024:1536].rearrange("p (i c) -> p i c", i=4), tCf[:, 1536:2048].rearrange("p (i c) -> p i c", i=4)]
            QD("sp", ("s5b", 0), [], t4, [(Bp[0], bp_d[l, 0][:, 4 * hf:4 * hf + 4, :]), (Bp[1], bp_d[l, 1][:, 4 * hf:4 * hf + 4, :])])
            for ii in range(4):
                i = 4 * hf + ii
                kre = s5s[:, 8, i:i + 1]
                kim = s5s[:, 9, i:i + 1]
                Q("dve", t4 + [tok], t4, lambda e, ii=ii, kim=kim, Bp=Bp, Bb=Bb: e.tensor_scalar(
                    out=Bb[0][:, ii, :], in0=Bp[1][:, ii, :], scalar1=kim, scalar2=None, op0=ALU.mult))
                Q("dve", t4 + [tok], t4, lambda e, ii=ii, kre=kre, Bp=Bp, Bb=Bb: e.scalar_tensor_tensor(
                    out=Bb[0][:, ii, :], in0=Bp[0][:, ii, :], scalar=kre, in1=Bb[0][:, ii, :], op0=ALU.mult, op1=ALU.subtract))
                Q("dve", t4 + [tok], t4, lambda e, ii=ii, kim=kim, Bp=Bp, Bb=Bb: e.tensor_scalar(
                    out=Bb[1][:, ii, :], in0=Bp[0][:, ii, :], scalar1=kim, scalar2=None, op0=ALU.mult))
                Q("dve", t4 + [tok], t4, lambda e, ii=ii, kre=kre, Bp=Bp, Bb=Bb: e.scalar_tensor_tensor(
                    out=Bb[1][:, ii, :], in0=Bp[1][:, ii, :], scalar=kre, in1=Bb[1][:, ii, :], op0=ALU.mult, op1=ALU.add))
            for ri in range(2):
                bank = 6
                cur.update(gate=2)

                def ftr(e, ri=ri, bank=bank, Bb=Bb):
                    ins = None
                    for ii in range(4):
                        ins = e.transpose(psb[bank][:, ii * 128:(ii + 1) * 128], Bb[ri][:, ii, :], ident_f)
                    return ins
                Q("pe", t4 + ["cst"], [tps(bank)], ftr)
                Q("act", [tps(bank)], [("RA", 2)], lambda e, ri=ri, hf=hf, bank=bank: e.activation(
                    out=BbT[ri][:, 4 * hf:4 * hf + 4, :], in_=psb[bank][:].rearrange("p (a b) -> p a b", a=4), func=AF.Copy))
            cur.update(gate=None)
        QD("sp", ("s5b", 2), [], t4, [(tC[:, h, 0:128], sguw_d[l, h]) for h in range(4)])

        def ftw(e):
            ins = None
            for h in range(4):
                ins = e.transpose(psb[6][:, h * 128:(h + 1) * 128], tC[:, h, 0:128], ident_f)
            return ins
        Q("pe", t4 + ["cst"], [tps(6)], ftw)
        Q("dve", [tps(6), "cst"], ["WsT"], lambda e: e.tensor_tensor(
            out=WsT[:], in0=psb[6][:].rearrange("p (a b) -> p a b", a=4),
            in1=trimask.unsqueeze(1).to_broadcast([128, 4, 128]), op=ALU.mult))
        return lists

    class Pump:
        ORDER = ("free", "fold", "mat", "tab0", "tab1", "c256")

        def __init__(self, lists):
            self.l = lists
            self.free = {0: False, 1: False, 2: False}

        def step(self, n):
            for _ in range(n):
                done = True
                for nm in self.ORDER:
                    q = self.l[nm]
                    if not q:
                        continue
                    if nm == "c256" and (self.l["tab0"] or self.l["tab1"]):
                        continue
                    gate, th_ = q[0]
                    if gate is not None and not self.free[gate]:
                        continue
                    q.pop(0)
                    th_()
                    done = False
                    break
                if done:
                    return

        def flush(self):
            self.free = {0: True, 1: True, 2: True}
            while any(self.l[k] for k in self.ORDER):
                self.step(1000)

    def tap(nm, src_ap, reads):
        if nm in taps_d:
            P.dma("pool", ("tap", nm), reads, [], [(taps_d[nm], src_ap)])

    def mixer_s5(l):
        par = l % 2
        slot = RBr.next(("mi", l, "D"))
        W = 256
        tok = ("s5s",)
        RT = [("RA", 0), ("RA", 1)]
        bc = lambda ap: ap.unsqueeze(1).to_broadcast([128, 2, W])
        ybank = lambda s8: 5 if s8 % 2 == 0 else 6
        tAb = tA[:].bitcast(BF16)
        tBf = tB[:].rearrange("p a b -> p (a b)")
        ubv = tC[:, 2, :].bitcast(BF16)
        ygv = tC[:, 3, :].bitcast(BF16)
        ub = lambda ct: ubv[:, ct * 256:ct * 256 + 256]
        ygb = lambda ct: ygv[:, ct * 256:ct * 256 + 256]

        def wbuf(k):
            return tA[:, k * 512:k * 512 + 256], tA[:, k * 512 + 256:k * 512 + 512], [("tA8", 2 * k), ("tA8", 2 * k + 1)]

        def abuf(k):
            o = 2048 + k * 1024
            return tAb[:, o:o + 512], tAb[:, o + 512:o + 1024], [("tA8", 4 + 2 * k), ("tA8", 5 + 2 * k)]

        rsb = rs[:].rearrange("p a b -> p (a b)").bitcast(BF16)

        def pbuf(k):
            if k < 2:
                o = k * 1024
                return tBf[:, o:o + 512], tBf[:, o + 512:o + 1024], [("tB", 2 * k), ("tB", 2 * k + 1)]
            return rsb[:, 0:512], rsb[:, 512:1024], [("rs", 0), ("rs", 0)]

        def prologue(s8):
            t0 = s8 * W
            hs = s8 // 2
            for ct in range(2):
                mm_cols(slot, ct * 128, lambda k, t0=t0: hT[:, k, t0:t0 + W], psb[4][:, ct * 256:ct * 256 + 256],
                        [th(k, hs) for k in range(8)], tps(4))
            P.op("act", [tps(4)], [("tC", 2)], lambda e: e.activation(out=ubv[:, 0:512], in_=psb[4][:], func=AF.Copy))

        def ystart(s8, ct):
            yb = ybank(s8)
            P.op("pe", ["diagD", ("tC", 2)], [tps(yb)], lambda e, ct=ct, yb=yb: e.matmul(
                psb[yb][:, ct * 256:ct * 256 + 256], diagD[:, ct, :], ub(ct), start=True, stop=False))

        def SA(s8, i):
            ct = i // 4
            k = i % 2
            bub = [psb[k][:, 0:256], psb[k][:, 256:512]]
            for ri in range(2):
                P.op("pe", [("RA", 2), ("tC", 2)], [tps(k)], lambda e, ri=ri, i=i, ct=ct, bub=bub: e.matmul(
                    bub[ri], BbT[ri][:, i, :], ub(ct), start=True, stop=True))
            tabi = tab[:, i]
            A_, Bs_, atok = abuf(k)
            P.op("dve", [tps(k)] + RT, [atok[0]], lambda e, tabi=tabi, bub=bub, A_=A_: e.tensor_tensor(
                out=A_.rearrange("p (c t) -> p c t", c=2), in0=tabi, in1=bc(bub[0]), op=ALU.mult))
            P.op("dve", [tps(k)] + RT, [atok[1]], lambda e, tabi=tabi, bub=bub, Bs_=Bs_: e.tensor_tensor(
                out=Bs_[:, 0:256], in0=tabi[:, 1, :], in1=bub[1], op=ALU.mult))
            P.op("dve", [tps(k)] + RT, [atok[1]], lambda e, tabi=tabi, bub=bub, Bs_=Bs_: e.scalar_tensor_tensor(
                out=Bs_[:, 256:512], in0=tabi[:, 0, :], scalar=-1.0, in1=bub[1], op0=ALU.mult, op1=ALU.mult))

        def SB(s8, i):
            k = i % 2
            A_, Bs_, atok = abuf(k)
            vb = 2 + k

            def fadd(e, A_=A_, Bs_=Bs_, vb=vb):
                e.matmul(psb[vb][:], ident_b[:], A_, start=True, stop=False)
                return e.matmul(psb[vb][:], ident_b[:], Bs_, start=False, stop=True)
            P.op("pe", atok + ["ident_b"], [tps(vb)], fadd)
            wre_b, wn_b, wtok = wbuf(k)
            rbc = s5s[:, 2, i:i + 1].to_broadcast([128, W])
            P.op("dve", [tps(vb), tok, ("s5i", i)], [wtok[0]], lambda e, rbc=rbc, i=i, wre_b=wre_b, vb=vb: e.tensor_tensor_scan(
                out=wre_b, data0=rbc, data1=psb[vb][:, 0:256], initial=s5i[:, 0, i:i + 1], op0=ALU.mult, op1=ALU.add))
            P.op("dve", [tps(vb), tok, ("s5i", i)], [wtok[1]], lambda e, rbc=rbc, i=i, wn_b=wn_b, vb=vb: e.tensor_tensor_scan(
                out=wn_b, data0=rbc, data1=psb[vb][:, 256:512], initial=s5i[:, 1, i:i + 1], op0=ALU.mult, op1=ALU.add))
            wre = wre_b[:, W - 1:W]
            wn = wn_b[:, W - 1:W]
            c256 = s5s[:, 12, i:i + 1]
            s256 = s5s[:, 13, i:i + 1]
            ns256 = s5s[:, 14, i:i + 1]
            ta_ = smal[:, 32:33]
            tb_ = smal[:, 33:34]
            P.op("act", [wtok[1], tok], [("smal",)], lambda e, wn=wn, s256=s256: e.activation(
                out=ta_, in_=wn, func=AF.Identity, scale=s256))
            P.op("act", [wtok[0], tok, ("smal",)], [("s5i", i)], lambda e, wre=wre, c256=c256, i=i: e.activation(
                out=s5i[:, 0, i:i + 1], in_=wre, func=AF.Identity, scale=c256, bias=ta_))
            P.op("act", [wtok[1], tok], [("smal",)], lambda e, wn=wn, c256=c256: e.activation(
                out=tb_, in_=wn, func=AF.Identity, scale=c256))
            P.op("act", [wtok[0], tok, ("smal",)], [("s5i", i)], lambda e, wre=wre, ns256=ns256, i=i: e.activation(
                out=s5i[:, 1, i:i + 1], in_=wre, func=AF.Identity, scale=ns256, bias=tb_))
            tabi = tab[:, i]
            Ap_, Bp_, ptok = pbuf((s8 * 8 + i) % 3)
            P.op("pool", [wtok[0]] + RT, [ptok[0]], lambda e, tabi=tabi, wre_b=wre_b, Ap_=Ap_: e.tensor_tensor(
                out=Ap_.rearrange("p (c t) -> p c t", c=2), in0=tabi, in1=bc(wre_b), op=ALU.mult))
            P.op("pool", [wtok[1]] + RT, [ptok[1]], lambda e, tabi=tabi, wn_b=wn_b, Bp_=Bp_: e.tensor_tensor(
                out=Bp_.rearrange("p (c t) -> p c t", c=2), in0=tabi, in1=bc(wn_b), op=ALU.mult))

        def SC(s8, i):
            ct = i // 4
            k = i % 2
            if i == 4:
                ystart(s8, 1)
            yb = ybank(s8)
            yap = psb[yb][:, ct * 256:ct * 256 + 256]
            Ap_, Bp_, ptok = pbuf((s8 * 8 + i) % 3)
            last = (i % 4 == 3)

            def fc(e, i=i, yap=yap, Ap_=Ap_, Bp_=Bp_, last=last):
                e.matmul(yap, CTb[0][:, i, :], Ap_[:, 0:256], start=False, stop=False)
                e.matmul(yap, CTb[0][:, i, :], Bp_[:, 256:512], start=False, stop=False)
                e.matmul(yap, nCTim[:, i, :], Ap_[:, 256:512], start=False, stop=False)
                return e.matmul(yap, CTb[1][:, i, :], Bp_[:, 0:256], start=False, stop=last)
            P.op("pe", [("RA", 2), "nCTim"] + ptok, [tps(yb)], fc)

        def epilogue(s8, part):
            yb = ybank(s8)
            yg = [tC[:, 0, 0:W], tC[:, 1, 0:W]]
            yd = [tC[:, 0, 256:256 + W], tC[:, 1, 256:256 + W]]
            if part == 0:
              for ct in range(2):
                P.op("act", [tps(yb)], [("tC", ct)], lambda e, ct=ct: e.activation(
                    out=yg[ct], in_=psb[yb][:, ct * 256:ct * 256 + 256], func=AF.Gelu_apprx_tanh))
                P.op("act", [("tC", ct)], [("tC", 3)], lambda e, ct=ct: e.activation(out=ygb(ct), in_=yg[ct], func=AF.Copy))
              if l == 0 and s8 == 0:
                tap("s5y", tC[:, 0, 0:W], [("tC", 0)])
              return
            if part == 1:
              for co in range(2):
                gap = psb[4][:, co * 256:co * 256 + 256]

                def fg(e, co=co, gap=gap):
                    ins = None
                    for ct in range(2):
                        ins = e.matmul(gap, gluw[:, ct, co * 128:(co + 1) * 128], ygb(ct), start=(ct == 0), stop=(ct == 1))
                    return ins
                P.op("pe", ["gluw", ("tC", 3)], [tps(4)], fg)
              return
            if part == 2:
              for co in range(2):
                gap = psb[4][:, co * 256:co * 256 + 256]
                P.op("act", [tps(4), ("pvec", par)], [("tC", co)], lambda e, co=co, gap=gap: e.activation(
                    out=yd[co], in_=gap, func=AF.Sigmoid, bias=pvec[:, par, PV_GB + co:PV_GB + co + 1], scale=1.0))
              return
            if part == 3:
              for co in range(2):
                P.op("dve", [("tC", co)], [("tC", co)], lambda e, co=co: e.tensor_tensor(
                    out=yd[co], in0=yd[co], in1=yg[co], op=ALU.mult))
              for ct in range(2):
                P.op("act", [("tC", ct)], [("g5sq", ct)], lambda e, ct=ct: e.activation(out=g5sq[:, ct, :], in_=yd[ct], func=AF.Square))
              return
            if part == 4:
              for ct in range(2):
                P.op("pe", [("g5sq", ct), "ones_b"], [tps(4)], lambda e, ct=ct: e.matmul(
                    psb[4][:, 0:W], ones_b[:], g5sq[:, ct, :], start=(ct == 0), stop=(ct == 1)))
              P.op("act", [tps(4), "epsc"], [("g5rs",)], lambda e: e.activation(
                out=g5rs[:], in_=psb[4][:, 0:W], func=AF.Ln, bias=epsc[:, 0:1], scale=1.0 / 256))
              P.op("act", [("g5rs",)], [("g5rs",)], lambda e: e.activation(out=g5rs[:], in_=g5rs[:], func=AF.Exp, scale=-0.5))
              return
            for ct in range(2):
                P.op("dve", [("tC", ct), ("g5rs",), ("pvec", par)], [tg(6 + ct, (s8 * W) // 512)], lambda e, ct=ct: e.scalar_tensor_tensor(
                    out=G[:, 6 + ct, s8 * W:(s8 + 1) * W], in0=yd[ct], scalar=pvec[:, par, PV_MNG + 6 + ct:PV_MNG + 6 + ct + 1],
                    in1=g5rs[:], op0=ALU.mult, op1=ALU.mult))

        prologue(0)
        NG = 64
        ep_sched = {}
        EP_OFF = [11, 13, 14, 15, 16, 18]
        for s8 in range(8):
            for part in range(6):
                ep_sched.setdefault(s8 * 8 + EP_OFF[part], []).append((s8, part))
        for g in range(NG + 20):
            if g < NG:
                s8, i = divmod(g, 8)
                if i == 3:
                    pass
                SA(s8, i)
            if 0 <= g - 1 < NG:
                s8, i = divmod(g - 1, 8)
                SB(s8, i)
            if 0 <= g - 3 < NG:
                s8, i = divmod(g - 3, 8)
                if i == 0:
                    ystart(s8, 0)
                SC(s8, i)
            if g < NG and g % 8 == 7 and g + 1 < NG:
                prologue(g // 8 + 1)
            for (s8, part) in ep_sched.get(g, []):
                epilogue(s8, part)

    def mixer_sgu(l):
        par = l % 2
        slot_u = RBr.next(("mi", l, "Au"))
        slot_v = RBr.next(("mi", l, "Av"), la=0)
        for s in range(4):
            ug = [tAv(0), tAv(1)]
            for ct in range(2):
                mm_cols(slot_u, ct * 128, lambda k, s=s: hT[:, k, sl(s)], psb[ct][:], [th(k, s) for k in range(8)], tps(ct))
                P.op("act", [tps(ct)], [("tA", ct)], lambda e, ct=ct: e.activation(out=ug[ct], in_=psb[ct][:], func=AF.Gelu_apprx_tanh))
            v3 = lambda ap: ap.rearrange("p (h d) -> p h d", h=4)
            cen_of = lambda nl: tC[:, 1 + nl // 2, (nl % 2) * 256:(nl % 2) * 256 + 256]
            cen_tok = lambda nl: ("tC", 1 + nl // 2)
            smt = ("smal", 0)
            for nl in range(4):
                n = 4 * s + nl
                vb = 2 + nl // 2
                zv = psb[vb][:, (nl % 2) * 256:(nl % 2) * 256 + 256]
                vg = tC[:, 0, (nl % 2) * 256:(nl % 2) * 256 + 256]
                cen = cen_of(nl)
                sq = tC[:, 3, 0:256]

                def fv(e, n=n, zv=zv):
                    ins = None
                    for kk in range(8):
                        ins = e.matmul(zv, hT[:, kk, n * 128:(n + 1) * 128], RB[:, slot_v, kk, :], start=(kk == 0), stop=(kk == 7))
                    return ins
                P.op("pe", [("RB", slot_v)] + [th(kk, s) for kk in range(8)], [tps(vb)], fv)
                P.op("act", [tps(vb)], [("tC", 0)], lambda e, zv=zv, vg=vg: e.activation(out=vg, in_=zv, func=AF.Gelu_apprx_tanh))
                P.op("dve", [("tC", 0)], [smt], lambda e, vg=vg, nl=nl: e.tensor_reduce(
                    out=smal[:, 4 * nl:4 * nl + 4], in_=v3(vg), axis=AX.X, op=ALU.add))
                P.op("dve", [("tC", 0), smt], [cen_tok(nl)], lambda e, vg=vg, cen=cen, nl=nl: e.scalar_tensor_tensor(
                    out=v3(cen), in0=smal[:, 4 * nl:4 * nl + 4].unsqueeze(2).to_broadcast([128, 4, 64]), scalar=-1.0 / 64,
                    in1=v3(vg), op0=ALU.mult, op1=ALU.add))
                P.op("dve", [cen_tok(nl)], [("tC", 3)], lambda e, cen=cen, sq=sq: e.tensor_tensor(out=sq, in0=cen, in1=cen, op=ALU.mult))
                P.op("dve", [("tC", 3)], [("smal", 1)], lambda e, sq=sq, nl=nl: e.tensor_reduce(
                    out=smal[:, 16 + 4 * nl:16 + 4 * nl + 4], in_=v3(sq), axis=AX.X, op=ALU.add))
                if l == 0 and n == 0:
                    tap("sguvn", cen, [cen_tok(nl)])
            P.op("act", [("smal", 1), "epsc"], [("smal", 1)], lambda e: e.activation(
                out=smal[:, 16:32], in_=smal[:, 16:32], func=AF.Ln, bias=epsc[:, 0:1], scale=1.0 / 64))
            P.op("act", [("smal", 1)], [("smal", 1)], lambda e: e.activation(
                out=smal[:, 16:32], in_=smal[:, 16:32], func=AF.Exp, scale=-0.5))
            for nl in range(4):
                cen = cen_of(nl)
                vn = tB_slot()
                P.op("dve", [cen_tok(nl), ("smal", 1)], [("tB", vn)], lambda e, vn=vn, cen=cen, nl=nl: e.tensor_tensor(
                    out=v3(tB[:, vn, 0:256]), in0=v3(cen),
                    in1=smal[:, 16 + 4 * nl:16 + 4 * nl + 4].unsqueeze(2).to_broadcast([128, 4, 64]), op=ALU.mult))

                def fm(e, vn=vn, nl=nl):
                    ins = None
                    for h in range(4):
                        ct = h // 2
                        po = (h % 2) * 64
                        ins = e.matmul(psb[4 + ct][po:po + 64, nl * 128:(nl + 1) * 128], tB[:, vn, h * 64:(h + 1) * 64],
                                       WsT[:, h, :], start=True, stop=True)
                    return ins
                P.op("pe", [("tB", vn), "WsT"], [tps(4), tps(5)], fm)
            ya = [tC[:, 2, 0:512], tC[:, 3, 0:512]]
            for ct in range(2):
                P.op("dve", [tps(4 + ct), "bsT"], [("tC", 2 + ct)], lambda e, ct=ct: e.tensor_tensor(
                    out=ya[ct].rearrange("p (a b) -> p a b", a=4), in0=psb[4 + ct][:].rearrange("p (a b) -> p a b", a=4),
                    in1=bsT[:, ct, :].unsqueeze(1).to_broadcast([128, 4, 128]), op=ALU.add))
                P.op("dve", [("tC", 2 + ct), ("tA", ct)], [("tC", 2 + ct)], lambda e, ct=ct: e.tensor_tensor(
                    out=ya[ct], in0=ya[ct], in1=ug[ct], op=ALU.mult))
            if l == 0 and s == 0:
                tap("sguya", ya[0], [("tC", 2)])
            gnorm(l, 0, ya, [("tC", 2), ("tC", 3)], s, 512)
        RBr.prefetch()

    def mixer_pool(l):
        par = l % 2
        slot = RBr.next(("mi", l, "B"))
        zb = [tC[:, 0, :], tC[:, 1, :]]
        pa = tA[:, 0:528]
        pb = tA[:, 528:1056]
        pc = tA[:, 1056:1584]
        ptoks = [("tA", 0), ("tA", 1), ("tA", 2), ("tA", 3)]
        invw = cst[:, C_IW:C_IW + 2]
        icnt = cst[:, C_ICNT:C_ICNT + 32].rearrange("p (a b) -> p a b", a=2)
        for ct in range(2):
            P.op("dve", [], [("tC", ct)], lambda e, ct=ct: e.memset(zb[ct][:, 0:16], 0.0))
        for s in range(4):
            yb = [tC[:, 2, 0:512], tC[:, 3, 0:512]]
            for ct in range(2):
                mm_cols(slot, ct * 128, lambda k, s=s: hT[:, k, sl(s)], psb[ct][:], [th(k, s) for k in range(8)], tps(ct))
                P.op("act", [tps(ct)], [("tC", ct)], lambda e, ct=ct: e.activation(out=zb[ct][:, 16:528], in_=psb[ct][:], func=AF.Copy))
                z = zb[ct]
                P.op("dve", [("tC", ct)], ptoks, lambda e, z=z: e.tensor_tensor(out=pa[:, 2:528], in0=z[:, 2:528], in1=z[:, 1:527], op=ALU.add))
                P.op("dve", ptoks, ptoks, lambda e: e.tensor_tensor(out=pb[:, 4:528], in0=pa[:, 4:528], in1=pa[:, 2:526], op=ALU.add))
                if ct == 0:
                    P.op("dve", ptoks, ptoks, lambda e: e.tensor_copy(out=pb[0:64, 16:528], in_=pa[0:64, 16:528]))
                    ws = pb
                else:
                    P.op("dve", ptoks, ptoks, lambda e: e.tensor_tensor(out=pc[:, 8:528], in0=pb[:, 8:528], in1=pb[:, 4:524], op=ALU.add))
                    P.op("dve", ptoks, ptoks, lambda e: e.tensor_tensor(out=pa[64:128, 16:528], in0=pc[64:128, 16:528], in1=pc[64:128, 8:520], op=ALU.add))
                    P.op("dve", ptoks, ptoks, lambda e: e.tensor_copy(out=pa[0:64, 16:528], in_=pc[0:64, 16:528]))
                    ws = pa
                pbf = tB_slot()
                P.op("dve", ptoks + [("tC", ct), "cst"], [("tB", pbf)], lambda e, ws=ws, z=z, ct=ct, pbf=pbf: e.scalar_tensor_tensor(
                    out=tB[:, pbf, :], in0=ws[:, 16:528], scalar=invw[:, ct:ct + 1], in1=z[:, 16:528], op0=ALU.mult, op1=ALU.subtract))
                if s == 0:
                    P.op("dve", ptoks + ["cst"], ptoks, lambda e, ws=ws, ct=ct: e.tensor_tensor(
                        out=ws[:, 16:32], in0=ws[:, 16:32], in1=icnt[:, ct, :], op=ALU.mult))
                    P.op("dve", ptoks + [("tC", ct)], [("tB", pbf)], lambda e, ws=ws, z=z, pbf=pbf: e.tensor_tensor(
                        out=tB[:, pbf, 0:16], in0=ws[:, 16:32], in1=z[:, 16:32], op=ALU.subtract))
                P.op("dve", [("tC", ct)], [("tC", ct)], lambda e, z=z: e.tensor_copy(out=z[:, 0:16], in_=z[:, 512:528]))
                bank = 4 + ct
                P.op("pe", ["poolw", ("tB", pbf)], [tps(bank)], lambda e, ct=ct, pbf=pbf, bank=bank: e.matmul(
                    psb[bank][:], poolw[:, ct, :], tB[:, pbf, :], start=True, stop=True))
                P.op("act", [tps(bank), ("pvec", par)], [("tC", 2 + ct)], lambda e, ct=ct, bank=bank: e.activation(
                    out=yb[ct], in_=psb[bank][:], func=AF.Identity, scale=pvec[:, par, PV_PSC + ct:PV_PSC + ct + 1]))
            if l == 0 and s == 0:
                tap("poolyb", yb[0], [("tC", 2)])
            gnorm(l, 1, yb, [("tC", 2), ("tC", 3)], s, 512)

    def mixer_conv(l):
        par = l % 2
        slot_c = RBr.next(("mi", l, "Cc"))
        slot_x = RBr.next(("mi", l, "Cx"), la=0)
        slot_b = RBr.next(("mi", l, "Cb"), la=0)
        yb_ = [tC[:, 0, 0:514], tC[:, 1, 0:514]]
        for ct in range(2):
            P.op("dve", [], [("tC", ct)], lambda e, ct=ct: e.memset(yb_[ct][:, 0:2], 0.0))
        for s in range(4):
            yc = [tC[:, 2, 0:512], tC[:, 3, 0:512]]
            for ct in range(2):
                y = yb_[ct]
                cw = lambda k, ct=ct: pvec[:, par, PV_CW + 3 * ct + k:PV_CW + 3 * ct + k + 1]
                mm_cols(slot_x, ct * 128, lambda k, s=s: hT[:, k, sl(s)], psb[0][:], [th(k, s) for k in range(8)], tps(0))
                mm_cols(slot_c, ct * 128, lambda k, s=s: hT[:, k, sl(s)], psb[1][:], [th(k, s) for k in range(8)], tps(1))
                a = tA_slot()
                P.op("act", [tps(0)], [("tA", a)], lambda e, a=a: e.activation(out=tAv(a), in_=psb[0][:], func=AF.Copy))
                P.op("dve", [("tA", a), tps(1)], [("tC", ct)], lambda e, a=a, y=y: e.tensor_tensor(
                    out=y[:, 2:514], in0=tAv(a), in1=psb[1][:], op=ALU.mult))
                a2 = tA_slot()
                P.op("act", [("tC", ct), ("pvec", par)], [("tA", a2)], lambda e, a2=a2, y=y, cw=cw: e.activation(
                    out=tAv(a2), in_=y[:, 2:514], func=AF.Identity, scale=cw(2)))
                P.op("dve", [("tC", ct), ("tA", a2), ("pvec", par)], [("tA", a2)], lambda e, a2=a2, y=y, cw=cw: e.scalar_tensor_tensor(
                    out=tAv(a2), in0=y[:, 1:513], scalar=cw(1), in1=tAv(a2), op0=ALU.mult, op1=ALU.add))
                P.op("dve", [("tC", ct), ("tA", a2), ("pvec", par)], [("tA", a2)], lambda e, a2=a2, y=y, cw=cw: e.scalar_tensor_tensor(
                    out=tAv(a2), in0=y[:, 0:512], scalar=cw(0), in1=tAv(a2), op0=ALU.mult, op1=ALU.add))
                P.op("dve", [("tC", ct)], [("tC", ct)], lambda e, y=y: e.tensor_copy(out=y[:, 0:2], in_=y[:, 512:514]))
                mm_cols(slot_b, ct * 128, lambda k, s=s: hT[:, k, sl(s)], psb[2 + ct][:], [th(k, s) for k in range(8)], tps(2 + ct))
                P.op("dve", [("tA", a2), tps(2 + ct)], [("tC", 2 + ct)], lambda e, a2=a2, ct=ct: e.tensor_tensor(
                    out=yc[ct], in0=tAv(a2), in1=psb[2 + ct][:], op=ALU.mult))
            if l == 0 and s == 0:
                tap("convyc", yc[0], [("tC", 2)])
            gnorm(l, 2, yc, [("tC", 2), ("tC", 3)], s, 512)
        RBr.prefetch()

    def mix_out(l, after_slice=None):
        par = l % 2
        slots4 = [RBr.next(("mo", l, jp), la=(3 if jp == 0 else 0)) for jp in range(4)]
        for s in range(4):
            for jp in range(4):
                slot = slots4[jp]
                for ji in range(2):
                    j = 2 * jp + ji
                    bank = out_bank()

                    def fo(e, slot=slot, s=s, ji=ji, bank=bank):
                        ins = None
                        for kk in range(8):
                            ins = e.matmul(psb[bank][:], RB[:, slot, kk, ji * 128:(ji + 1) * 128], G[:, kk, sl(s)],
                                           start=(kk == 0), stop=(kk == 7))
                        return ins
                    P.op("pe", [("RB", slot)] + [tg(kk, s) for kk in range(8)], [tps(bank)], fo)
                    P.op("dve", [tps(bank), tx(j, s), ("gate", par, 1)], [tx(j, s)],
                         lambda e, bank=bank, j=j, s=s: e.scalar_tensor_tensor(
                             out=xT[:, j, sl(s)], in0=psb[bank][:], scalar=gate[:, par, 1, j:j + 1],
                             in1=xT[:, j, sl(s)], op0=ALU.mult, op1=ALU.add))
            if after_slice is not None:
                after_slice(s)
        RBr.prefetch()

    out_evs = []
    hTf = hT[:].rearrange("p a b -> p (a b)").bitcast(F32)

    def final_p2(s, r):
        for jg in range(2):
            slots = []
            for jj in range(4):
                j = 4 * jg + jj
                a = tA_slot()
                slots.append(a)
                P.op("dve", [tx(j, s), ("rs", r), "fng"], [("tA", a)], lambda e, j=j, s=s, a=a, r=r: e.scalar_tensor_tensor(
                    out=tAv(a), in0=xT[:, j, sl(s)], scalar=fng[:, j:j + 1], in1=rs[:, r, :], op0=ALU.mult, op1=ALU.mult))
            for tl in range(4):
                tt = 4 * s + tl
                q = tt % 8
                stage = hTf[:, q * 1024:(q + 1) * 1024]
                bank = out_bank()

                def fe(e, slots=slots, tl=tl, bank=bank):
                    ins = None
                    for jj in range(4):
                        ins = e.transpose(psb[bank][:, jj * 128:(jj + 1) * 128], tAv(slots[jj])[:, tl * 128:(tl + 1) * 128], ident_f)
                    return ins
                P.op("pe", [("tA", a) for a in slots] + ["cst"], [tps(bank)], fe)
                wt = [("ostage", q, jg)] + [th(q, ss) for ss in range(4)]
                if (tt + jg) % 2 == 0:
                    P.op("act", [tps(bank)], wt, lambda e, stage=stage, jg=jg, bank=bank: e.activation(
                        out=stage[:, jg * 512:(jg + 1) * 512], in_=psb[bank][:], func=AF.Copy))
                else:
                    P.op("dve", [tps(bank)], wt, lambda e, stage=stage, jg=jg, bank=bank: e.tensor_copy(
                        out=stage[:, jg * 512:(jg + 1) * 512], in_=psb[bank][:]))
        for tl in range(4):
            tt = 4 * s + tl
            q = tt % 8
            stage = hTf[:, q * 1024:(q + 1) * 1024]
            ev = P.dma("sp", ("ost", q), [("ostage", q, 0), ("ostage", q, 1)] + [th(q, ss) for ss in range(4)], [],
                       [(out_d[tt * 128:(tt + 1) * 128, :], stage)])
            out_evs.append(ev)

    def final_cb():
        st = {}

        def cb(s):
            st[s] = stats_rstd(lambda k, s=s: xT[:, k, sl(s)], [tx(k, s) for k in range(8)], 8, 1.0 / D)
            if s >= 1:
                final_p2(s - 1, st[s - 1])
            if s == 3:
                final_p2(3, st[3])
        return cb

    for l in range(n_layers):
        P.new_epoch()
        if l + 1 < n_layers:
            load_layer_params(l + 1)
        if l == 0:
            norm_mod(l, 0)
            tap("h1", hT[:, 0, 0:512], [th(0, 0)])
        ffn(l, 0, after_slice=norm_cb(l, 1), pump=Pump(s5_setup_early(l)))
        if l == 0:
            tap("x1", xT[:, 0, 0:512], [tx(0, 0)])
        mixer_s5(l)
        mixer_sgu(l)
        mixer_pool(l)
        mixer_conv(l)
        if l == 0:
            tap("gall", G[:, :, 0:256], [tg(m, 0) for m in range(8)])
        mix_out(l, after_slice=norm_cb(l, 2))
        if l == 0:
            tap("x2", xT[:, 0, 0:512], [tx(0, 0)])
        if l + 1 < n_layers:
            ffn(l, 1, after_slice=norm_cb(l + 1, 0))
        else:
            ffn(l, 1, after_slice=final_cb())
        if l == 0:
            tap("x3", xT[:, 0, 0:512], [tx(0, 0)])

    for nm in taps_d:
        out_evs.append((None, ("D", ("tap", nm)), P.dcnt[("D", ("tap", nm))]))
    P.wait_all("sp", out_evs)

    sems = {}
    for i, sk in enumerate(P.semkeys):
        sems[sk] = es.enter_context(nc.semaphore("s%d" % i))
    with nc.Block() as block:
        P.emit(nc, block, sems)
    es.close()
    return nc


def _colmajor(v, ncols):
    return np.ascontiguousarray(np.swapaxes(v.reshape(v.shape[:-1] + (ncols, 128)), -1, -2))


def prepare_inputs(inp):
    f32 = np.float32
    g = {k: np.asarray(v, dtype=f32) for k, v in inp.items()}
    shared = {}
    for k in ["ada_w", "ffn1_w_in", "ffn1_w_out", "ffn2_w_in", "ffn2_w_out", "w_mix_in", "w_mix_out", "sgu_w"]:
        shared[k] = np.ascontiguousarray(g[k])
    shared["glu_w"] = np.ascontiguousarray(g["s5_glu_w"])
    shared["ada_bT"] = _colmajor(g["ada_b"], 72)
    shared["fng"] = _colmajor(g["final_norm_g"], 8)
    pv = np.zeros((L, 128, NPV), f32)
    pv[:, :, PV_N1:PV_N1 + 8] = _colmajor(g["norm1_g"], 8)
    pv[:, :, PV_N2:PV_N2 + 8] = _colmajor(g["norm2_g"], 8)
    pv[:, :, PV_N3:PV_N3 + 8] = _colmajor(g["norm3_g"], 8)
    pv[:, :, PV_MNG:PV_MNG + 8] = _colmajor(g["mix_norm_g"], 8)
    pv[:, :, PV_PSC:PV_PSC + 2] = _colmajor(g["pool_scale"], 2)
    cw = g["conv_w"]
    for ct in range(2):
        for k in range(3):
            pv[:, :, PV_CW + 3 * ct + k] = cw[:, k, ct * 128:(ct + 1) * 128]
    pv[:, :, PV_D:PV_D + 2] = _colmajor(g["s5_d"], 2)
    pv[:, :, PV_GB:PV_GB + 2] = _colmajor(g["s5_glu_b"], 2)
    pv[:, :, PV_LRE:PV_LRE + 8] = _colmajor(g["s5_lambda_re"].reshape(L, 1024), 8)
    pv[:, :, PV_LIM:PV_LIM + 8] = _colmajor(g["s5_lambda_im"].reshape(L, 1024), 8)
    ldt = np.repeat(g["s5_log_dt"], 64, axis=1)
    pv[:, :, PV_LDT:PV_LDT + 8] = _colmajor(ldt, 8)
    shared["pvec"] = pv
    sb_ = g["sgu_b"]
    sbt = np.zeros((L, 128, 2, 128), f32)
    for tile in range(2):
        for gl in range(2):
            sbt[:, gl * 64:(gl + 1) * 64, tile, :] = sb_[:, 2 * tile + gl, None, :]
    shared["sgu_bT"] = sbt
    pw = g["pool_w"]
    pbd = np.zeros((L, 128, 2, 128), f32)
    for tile in range(2):
        for gl in range(2):
            pbd[:, gl * 64:(gl + 1) * 64, tile, gl * 64:(gl + 1) * 64] = pw[:, 2 * tile + gl]
    shared["pool_bd"] = pbd
    bp = np.zeros((L, 2, 128, 8, 128), f32)
    cp = np.zeros((L, 2, 128, 8, 128), f32)
    for ri, (bn, cn) in enumerate([("s5_b_re", "s5_c_re"), ("s5_b_im", "s5_c_im")]):
        b = g[bn]
        c = g[cn]
        for i in range(8):
            for gl in range(2):
                gg = 2 * i + gl
                c0 = 32 * (i % 4) + 16 * gl
                bp[:, ri, gl * 64:(gl + 1) * 64, i, c0:c0 + 16] = b[:, gg]
                cp[:, ri, gl * 64:(gl + 1) * 64, i, c0:c0 + 16] = np.swapaxes(c[:, gg], -1, -2)
    shared["s5_Bp"] = bp
    shared["s5_Cp"] = cp
    cst = np.zeros((128, NCST), f32)
    cst[:, C_ID:C_ID + 128] = np.eye(128, dtype=f32)
    cst[:, C_TRI:C_TRI + 128] = np.triu(np.ones((128, 128), f32))
    cst[:, C_IOTA:C_IOTA + 256] = np.arange(256, dtype=f32)[None, :]
    wins = [2, 4, 8, 16]
    for tile in range(2):
        for gl in range(2):
            w = wins[2 * tile + gl]
            t = np.arange(16)
            cst[gl * 64:(gl + 1) * 64, C_ICNT + 16 * tile:C_ICNT + 16 * tile + 16] = 1.0 / np.minimum(t + 1, w)
            cst[gl * 64:(gl + 1) * 64, C_IW + tile] = 1.0 / w
    shared["cst"] = cst
    in_maps = []
    for b in range(8):
        m = dict(shared)
        m["x"] = np.ascontiguousarray(g["x"][b])
        m["cT"] = _colmajor(g["c"][b], 8)
        in_maps.append(m)
    return in_maps


_NC_CACHE = {}


def kernel(**inputs):
    in_maps = prepare_inputs(inputs)
    if "nc" not in _NC_CACHE:
        _NC_CACHE["nc"] = build_program()
    nc = _NC_CACHE["nc"]
    res = run_bass_kernel_spmd(nc, in_maps, core_ids=list(range(8)))
    out = np.stack([np.asarray(r["out"], dtype=np.float32) for r in res.results], axis=0)
    return out
```
